# Optimizing a Trainium2 kernel written in Bass

```python
import jax, jax.numpy as jnp
from jax import lax
import numpy as np

D_MODEL = 1024
BATCH = 2
SEQ = 8192
DEPTH = 1

PLE_DIM = 256
HEAD_DIM = 64
FOX_HEADS = 8
SB_HEADS = 8
FOX_WIDTH = FOX_HEADS * HEAD_DIM
SB_WIDTH = SB_HEADS * HEAD_DIM
D_FF = 2816
Q_BLOCK = 128
EPS = 1e-6
FORGET_BIAS_INIT = 2.0
IN_SIZES = (FOX_WIDTH, FOX_WIDTH, FOX_WIDTH, FOX_HEADS, SB_WIDTH, SB_WIDTH, SB_WIDTH, D_MODEL, D_MODEL)
IN_WIDTH = 3 * FOX_WIDTH + FOX_HEADS + 3 * SB_WIDTH + 2 * D_MODEL

kernel_name = "hybrid_fox_stickbreak_macaron_ple"


def rms_norm(x, g):
    xf = x.astype(jnp.float32)
    y = xf * lax.rsqrt(jnp.mean(xf * xf, axis=-1, keepdims=True) + EPS)
    return (y * g.astype(jnp.float32)).astype(x.dtype)


def swiglu(h, w_gate, w_up, w_down):
    return (jax.nn.silu(h @ w_gate) * (h @ w_up)) @ w_down


def to_blocks(t):
    b, s = t.shape[0], t.shape[1]
    return jnp.moveaxis(t.reshape(b, s // Q_BLOCK, Q_BLOCK, *t.shape[2:]), 1, 0)


def from_blocks(t):
    nb, b = t.shape[0], t.shape[1]
    return jnp.moveaxis(t, 0, 1).reshape(b, nb * Q_BLOCK, -1)


def forgetting_attention(q, k, v, log_f):
    s_len = q.shape[1]
    scale = HEAD_DIM ** -0.5
    F = jnp.cumsum(log_f, axis=1)
    Fk = jnp.transpose(F, (0, 2, 1))[:, :, None, :]
    kpos = jnp.arange(s_len)
    qpos = kpos.reshape(s_len // Q_BLOCK, Q_BLOCK)

    def one_block(args):
        qi, Fi, pi = args
        logits = jnp.einsum('bqhd,bkhd->bhqk', qi, k).astype(jnp.float32) * scale
        logits = logits + jnp.transpose(Fi, (0, 2, 1))[..., None] - Fk
        mask = pi[:, None] >= kpos[None, :]
        logits = jnp.where(mask, logits, -jnp.inf)
        w = jax.nn.softmax(logits, axis=-1)
        return jnp.einsum('bhqk,bkhd->bqhd', w.astype(v.dtype), v)

    out = lax.map(one_block, (to_blocks(q), to_blocks(F), qpos))
    return from_blocks(out)


def stick_breaking_attention(q, k, v):
    s_len = q.shape[1]
    scale = HEAD_DIM ** -0.5
    kpos = jnp.arange(s_len)
    qpos = kpos.reshape(s_len // Q_BLOCK, Q_BLOCK)

    def one_block(args):
        qi, pi = args
        z = jnp.einsum('bqhd,bkhd->bhqk', qi, k).astype(jnp.float32) * scale
        mask = kpos[None, :] < pi[:, None]
        log_beta = jax.nn.log_sigmoid(z)
        log_1m = jnp.where(mask, jax.nn.log_sigmoid(-z), 0.0)
        after = lax.cumsum(log_1m, axis=3, reverse=True) - log_1m
        a = jnp.where(mask, jnp.exp(log_beta + after), 0.0)
        return jnp.einsum('bhqk,bkhd->bqhd', a.astype(v.dtype), v)

    out = lax.map(one_block, (to_blocks(q), qpos))
    return from_blocks(out)


def split_columns(t):
    outs, start = [], 0
    for size in IN_SIZES:
        outs.append(t[..., start:start + size])
        start += size
    return outs


def setup_inputs(seed: int = 0) -> dict:
    key = jax.random.key(seed)
    ks = jax.random.split(key, 24)

    def w(k, shape, fan_in):
        return jax.random.normal(k, shape, jnp.float32) * (fan_in ** -0.5)

    def gain(k, shape):
        return 1.0 + 0.05 * jax.random.normal(k, shape, jnp.float32)

    L = DEPTH
    return {
        "x": jax.random.normal(ks[0], (BATCH, SEQ, D_MODEL), jnp.float32),
        "p": jax.random.normal(ks[1], (DEPTH, BATCH, SEQ, PLE_DIM), jnp.float32),
        "ffn1_norm": gain(ks[2], (L, D_MODEL)),
        "ffn1_w_gate": w(ks[3], (L, D_MODEL, D_FF), D_MODEL),
        "ffn1_w_up": w(ks[4], (L, D_MODEL, D_FF), D_MODEL),
        "ffn1_w_down": w(ks[5], (L, D_FF, D_MODEL), D_FF),
        "mix_norm": gain(ks[6], (L, D_MODEL)),
        "w_in": w(ks[7], (L, D_MODEL, IN_WIDTH), D_MODEL),
        "forget_bias": FORGET_BIAS_INIT + 0.1 * jax.random.normal(ks[8], (L, FOX_HEADS), jnp.float32),
        "q_norm": gain(ks[9], (L, HEAD_DIM)),
        "k_norm": gain(ks[10], (L, HEAD_DIM)),
        "w_branch_fox": w(ks[11], (L, FOX_WIDTH, D_MODEL), FOX_WIDTH),
        "w_branch_sb": w(ks[12], (L, SB_WIDTH, D_MODEL), SB_WIDTH),
        "w_out": w(ks[13], (L, D_MODEL, D_MODEL), D_MODEL),
        "ffn2_norm": gain(ks[14], (L, D_MODEL)),
        "ffn2_w_gate": w(ks[15], (L, D_MODEL, D_FF), D_MODEL),
        "ffn2_w_up": w(ks[16], (L, D_MODEL, D_FF), D_MODEL),
        "ffn2_w_down": w(ks[17], (L, D_FF, D_MODEL), D_FF),
        "ple_norm": gain(ks[18], (L, D_MODEL)),
        "w_ple_gate": w(ks[19], (L, D_MODEL, D_MODEL), D_MODEL),
        "w_ple_proj": w(ks[20], (L, PLE_DIM, D_MODEL), PLE_DIM),
    }


def reference(x, p, ffn1_norm, ffn1_w_gate, ffn1_w_up, ffn1_w_down, mix_norm, w_in,
              forget_bias, q_norm, k_norm, w_branch_fox, w_branch_sb, w_out,
              ffn2_norm, ffn2_w_gate, ffn2_w_up, ffn2_w_down, ple_norm,
              w_ple_gate, w_ple_proj):
    b, s_len, _ = x.shape
    for i in range(DEPTH):
        x = x + 0.5 * swiglu(rms_norm(x, ffn1_norm[i]), ffn1_w_gate[i], ffn1_w_up[i], ffn1_w_down[i])

        h = rms_norm(x, mix_norm[i])
        fq, fk, fv, f_logit, sq, sk, sv, g_fox, g_sb = split_columns(h @ w_in[i])
        fq = rms_norm(fq.reshape(b, s_len, FOX_HEADS, HEAD_DIM), q_norm[i])
        fk = rms_norm(fk.reshape(b, s_len, FOX_HEADS, HEAD_DIM), k_norm[i])
        fv = fv.reshape(b, s_len, FOX_HEADS, HEAD_DIM)
        log_f = jax.nn.log_sigmoid((f_logit + forget_bias[i]).astype(jnp.float32))
        y_fox = forgetting_attention(fq, fk, fv, log_f)

        sq = sq.reshape(b, s_len, SB_HEADS, HEAD_DIM)
        sk = sk.reshape(b, s_len, SB_HEADS, HEAD_DIM)
        sv = sv.reshape(b, s_len, SB_HEADS, HEAD_DIM)
        y_sb = stick_breaking_attention(sq, sk, sv)

        merged = (jax.nn.sigmoid(g_fox) * (y_fox @ w_branch_fox[i])
                  + jax.nn.sigmoid(g_sb) * (y_sb @ w_branch_sb[i]))
        x = x + merged @ w_out[i]

        x = x + 0.5 * swiglu(rms_norm(x, ffn2_norm[i]), ffn2_w_gate[i], ffn2_w_up[i], ffn2_w_down[i])

        x = x + jax.nn.sigmoid(rms_norm(x, ple_norm[i]) @ w_ple_gate[i]) * (p[i] @ w_ple_proj[i])
    return x
```

```python
import os
import numpy as np
import concourse.bass as bass
import concourse.mybir as mybir
from concourse.bass_utils import run_bass_kernel_spmd

F32 = mybir.dt.float32
BF16 = mybir.dt.bfloat16
AF = mybir.ActivationFunctionType
ALU = mybir.AluOpType
AX = mybir.AxisListType


class Buf:
    __slots__ = ("name", "w", "r", "ws")

    def __init__(self, name=""):
        self.name = name
        self.w = None
        self.r = {}
        self.ws = {}


class DmaSem:
    __slots__ = ("sem", "n", "sw")

    def __init__(self, sem):
        self.sem = sem
        self.n = 0
        self.sw = False


class Prog:
    ENGS = ("pe", "act", "dve", "pool", "sp")

    def __init__(self, nc, engsem):
        self.nc = nc
        self.engsem = engsem
        self.lists = {e: [] for e in self.ENGS}
        self.count = {e: 0 for e in self.ENGS}
        self.seen = {e: {} for e in self.ENGS}
        self.nwaits = 0

    def _wait(self, eng, tok):
        kind, key, val = tok
        if kind == "e" and key == eng and eng == "pe":
            return
        seen = self.seen[eng]
        if seen.get(key, 0) >= val:
            return
        seen[key] = val
        sem = self.engsem[key] if kind == "e" else key.sem
        self.nwaits += 1
        self.lists[eng].append(lambda e, sem=sem, val=val: e.wait_ge(sem, val))

    def _deps(self, eng, reads, writes):
        need = {}

        def add(tok):
            kind, key, val = tok
            if need.get(key, (None, None, 0))[2] < val:
                need[key] = tok

        for b in reads:
            if b.w is not None:
                add(b.w)
            for t in b.ws.values():
                add(t)
        for b in writes:
            if b.w is not None:
                add(b.w)
            for t in b.ws.values():
                add(t)
            for t in b.r.values():
                add(t)
        for tok in need.values():
            self._wait(eng, tok)

    def op(self, eng, fn, reads=(), writes=(), sig=True):
        self._deps(eng, reads, writes)
        if sig:
            self.count[eng] += 1
            tok = ("e", eng, self.count[eng])
            sem = self.engsem[eng]
            self.lists[eng].append(lambda e, fn=fn, sem=sem: fn(e).then_inc(sem, 1))
        else:
            tok = ("e", eng, self.count[eng] + 1)
            self.lists[eng].append(fn)
        for b in writes:
            b.w = tok
            b.r = {}
            b.ws = {}
        for b in reads:
            b.r[eng] = tok
        return tok

    def dma(self, q, out, in_, ds, reads=(), writes=(), inc=16, fn=None, multi=()):
        self._deps(q, reads, writes)
        for b in multi:
            for t in b.r.values():
                self._wait(q, t)
        ds.n += inc
        tok = ("d", ds, ds.n)
        sem = ds.sem
        if fn is None:
            fn = lambda e: e.dma_start(out=out, in_=in_)
        self.lists[q].append(lambda e, fn=fn, sem=sem, inc=inc: fn(e).then_inc(sem, inc))
        for b in writes:
            b.w = tok
            b.r = {}
            b.ws = {}
        for b in multi:
            b.ws[ds] = tok
        for b in reads:
            b.r[ds] = tok
        return tok

    def wait_tok(self, eng, tok):
        self._wait(eng, tok)

    def emit(self, block):
        L = self.lists

        @block.tensor
        def _(e):
            for f in L["pe"]:
                f(e)

        @block.scalar
        def _(e):
            for f in L["act"]:
                f(e)

        @block.vector
        def _(e):
            for f in L["dve"]:
                f(e)

        @block.gpsimd
        def _(e):
            for f in L["pool"]:
                f(e)

        @block.sync
        def _(e):
            for f in L["sp"]:
                f(e)

    def barrier(self, bar_sem, all_ds, excl=()):
        assert self.pe_open == 0
        all_ds = [d_ for d_ in all_ds if d_ not in excl]
        for e in self.ENGS:
            if e != "sp" and self.count[e] > 0:
                self._wait("sp", ("e", e, self.count[e]))
        for ds in all_ds:
            if ds.n > 0:
                self._wait("sp", ("d", ds, ds.n))
        self.nbar += 1
        n = self.nbar
        self.lists["sp"].append(lambda e, n=n: e.sem_inc(bar_sem, 1))
        for e in self.ENGS:
            if e == "sp":
                continue
            self.lists[e].append(lambda eng, n=n: eng.wait_ge(bar_sem, n))
        for e in self.ENGS:
            for e2 in self.ENGS:
                self.seen[e][e2] = self.count[e2]
            for ds in all_ds:
                self.seen[e][ds] = ds.n

    pe_open = 0
    nbar = 0


D = 1024
DFF = 2816
NF = 22
TOK = 2048
NT = 16
SEQ = 8192
NG = 4
EPS = 1e-6
C_FQ, C_FK, C_FV, C_FL, C_SQ, C_SK, C_SV, C_GF, C_GS = 0, 512, 1024, 1536, 1544, 2056, 2568, 3080, 4104
NEG = -30000.0
NSTG = 8
RG = [[0, 1, 2, 3], [4, 5, 6, 7]]


def build_program(debug=False, stop_after=None):
    from contextlib import ExitStack
    nc = bass.Bass("TRN2", target_bir_lowering=False)

    def din(name, shape, dt=F32):
        return nc.dram_tensor(name, list(shape), dt, kind="ExternalInput").ap()

    x_d = din("x", [TOK, D])
    p_d = din("p", [TOK, 256])
    wg_d = [din("wg1", [NF, 128, 1024]), din("wg2", [NF, 128, 1024])]
    wu_d = [din("wu1", [NF, 128, 1024]), din("wu2", [NF, 128, 1024])]
    wd_d = [din("wd1", [128, NF * 1024]), din("wd2", [128, NF * 1024])]
    wqk_d = din("wqk", [16, 128, 1024])
    wv_d = din("wv", [2, 128, 8 * 512])
    wfl_d = din("wfl", [128, 64])
    wgt_d = din("wgt", [16, 128, 1024])
    wbf_d = din("wbf", [128, 4 * 1024])
    wbs_d = din("wbs", [128, 4 * 1024])
    wout_d = din("wout", [128, 8 * 1024])
    wpg_d = din("wpg", [128, 8 * 1024])
    wpp_d = din("wpp", [128, 2 * 1024])
    gains_d = din("gains", [4, D])
    fb_d = din("fb", [8, 1])
    qkn_d = din("qkn", [128, 2])
    out_d = nc.dram_tensor("out", [TOK, D], F32, kind="ExternalOutput").ap()

    XTg = [[nc.dram_tensor("XT_%d_%d" % (g, h2), [1024, 512], BF16) for h2 in range(2)] for g in range(NG)]
    XVg = [nc.dram_tensor("XV_%d" % g, [512, 1024], BF16) for g in range(NG)]
    XL = nc.dram_tensor("XL", [8, TOK], F32)
    XTa = nc.dram_tensor("XTa", [8 * 4096, 512], BF16)
    XVa = nc.dram_tensor("XVa", [4 * 2048, 1024], BF16)
    XLa = nc.dram_tensor("XLa", [32, TOK], F32)
    YTc = [nc.dram_tensor("YT_%d" % q, [256, 2048], BF16) for q in range(4)]
    YTa = nc.dram_tensor("YTa", [4 * 1024, 2048], BF16)
    FP = nc.dram_tensor("FP", [2, 3, SEQ], BF16)
    MyQK = [nc.dram_tensor("MyQK%d" % h2, [128, 32, 512], BF16) for h2 in range(2)]
    MyV = nc.dram_tensor("MyV", [SEQ, 2, 128], BF16)
    MyL = nc.dram_tensor("MyL", [2, SEQ], F32)
    MyY = nc.dram_tensor("MyY", [1024, 2048], BF16)
    dbg = {}
    if debug:
        dbg["XT"] = nc.dram_tensor("dbg_XT", [2048, TOK], BF16, kind="ExternalOutput").ap()
        dbg["XV"] = nc.dram_tensor("dbg_XV", [TOK, 1024], BF16, kind="ExternalOutput").ap()
        dbg["XL"] = nc.dram_tensor("dbg_XL", [8, TOK], F32, kind="ExternalOutput").ap()
        dbg["YT"] = nc.dram_tensor("dbg_YT", [256, SEQ], BF16, kind="ExternalOutput").ap()
        dbg["x1"] = nc.dram_tensor("dbg_x1", [TOK, D], F32, kind="ExternalOutput").ap()
        dbg["x2"] = nc.dram_tensor("dbg_x2", [TOK, D], F32, kind="ExternalOutput").ap()
        dbg["x3"] = nc.dram_tensor("dbg_x3", [TOK, D], F32, kind="ExternalOutput").ap()

    es = ExitStack()
    with es:
        _uid = [0]

        def sb(name, shape, dt, stack=es):
            _uid[0] += 1
            if os.environ.get("SBDBG"):
                print("alloc", name, shape, dt, "remaining", nc.sbuf_bytes_remaining)
            return stack.enter_context(nc.sbuf_tensor("s%d_%s" % (_uid[0], name), list(shape), dt))

        def ps(name, shape, dt):
            return es.enter_context(nc.psum_tensor(name, list(shape), dt))

        sem_pool = [es.enter_context(nc.semaphore("sm%d" % i)) for i in range(94)]
        engsem = {e: sem_pool.pop() for e in Prog.ENGS}
        bar_sem = sem_pool.pop()
        P = Prog(nc, engsem)
        all_ds = []
        free_ds = []

        free_ds_sw = []

        def get_ds(sw=False):
            fl_ = free_ds_sw if sw else free_ds
            if fl_:
                return fl_.pop()
            ds = DmaSem(sem_pool.pop())
            ds.sw = sw
            all_ds.append(ds)
            return ds

        def put_ds(*dss):
            for d_ in dss:
                (free_ds_sw if d_.sw else free_ds).append(d_)

        _pid = {}

        def barrier(flush=True, excl=()):
            P.barrier(bar_sem, all_ds, excl)
            if flush:
                with nc.Block() as block:
                    P.emit(block)
                for k_ in P.lists:
                    P.lists[k_] = []
                _pid.clear()

        banks = [ps("bank%d" % i, [128, 512], F32) for i in range(7)]
        bankB = [Buf("bank%d" % i) for i in range(7)]
        trps = ps("trps", [128, 1024], BF16)
        trB = Buf("trps")

        xres = sb("xres", [128, NT, D], F32)
        xB = [Buf("x%d" % i) for i in range(NT)]
        gbc = sb("gbc", [128, 4, D], F32)
        gbcB = Buf("gbc")
        ident = sb("ident", [128, 128], BF16)
        cst_f = sb("cst_f", [128, 128], F32)
        ones_f = sb("ones_f", [128, 64], F32)
        negmask_fox = sb("nm_fox", [128, 128], BF16)
        negmask_sb = sb("nm_sb", [128, 128], BF16)
        negtri = sb("negtri", [128, 128], BF16)
        negones = sb("negones", [128, 128], BF16)
        blockones = sb("blockones", [128, 128], BF16)
        mhalf = sb("mhalf", [128, 8], F32)
        cst_eps = sb("cst_eps", [128, 1], F32)
        qkn = sb("qkn", [128, 2], F32)
        nfb = sb("nfb", [8, 1], F32)
        cB = Buf("consts")
        xnT = sb("xnT", [128, 8, 1024], BF16)
        xnTBs = [Buf("xnT0"), Buf("xnT1")]
        xn_t = [sb("xn%d" % i, [128, D], BF16) for i in range(2)]
        xn_B = [Buf("xn%d" % i) for i in range(2)]
        sqj = sb("sqj", [128, D], BF16)
        sqjB = Buf("sqj")
        stat = sb("stat", [128, 8], F32)
        statB = [Buf("stat%d" % i) for i in range(2)]
        stat2 = sb("stat2", [128, 8], F32)
        stat3 = sb("stat3", [128, 8], F32)

        ds_c = get_ds()
        def pool(fn, reads=(), writes=()):
            return P.op("pool", fn, reads, writes)

        def dve(fn, reads=(), writes=()):
            return P.op("dve", fn, reads, writes)

        def act(fn, reads=(), writes=()):
            return P.op("act", fn, reads, writes)

        def mm(out, lhsT, rhs, start, stop, reads, writes, sig):
            if sig:
                P.pe_open = 0
            else:
                P.pe_open = 1
            return P.op("pe", lambda e: e.matmul(out, lhsT=lhsT, rhs=rhs, start=start, stop=stop), reads, writes, sig=sig)

        ds_x = get_ds()
        xv = x_d.rearrange("(i p) d -> p i d", p=128)
        for q in range(4):
            P.dma("sp", xres[:, 4 * q:4 * q + 4, :], xv[:, 4 * q:4 * q + 4, :], ds_x, writes=xB[4 * q:4 * q + 4])
        P.dma("sp", gbc[:].rearrange("p a d -> p (a d)"), gains_d.rearrange("a d -> (a d)").partition_broadcast(128), ds_c, writes=[gbcB])
        cB2 = Buf("consts_dma")
        ds_c2 = get_ds()
        P.dma("sp", qkn[:], qkn_d, ds_c2, writes=[cB2])
        P.dma("sp", nfb[:], fb_d, ds_c2, multi=[cB2])

        pool(lambda e: e.memset(cst_f[:], 0.0), writes=[cB])
        pool(lambda e: e.affine_select(out=cst_f[:], in_=cst_f[:], pattern=[[-1, 128]], compare_op=ALU.not_equal, fill=1.0,
                                       base=0, channel_multiplier=1), reads=[cB], writes=[cB])
        pool(lambda e: e.tensor_copy(out=ident[:], in_=cst_f[:]), reads=[cB], writes=[cB])
        pool(lambda e: e.memset(cst_f[:], 0.0), reads=[cB], writes=[cB])
        pool(lambda e: e.affine_select(out=cst_f[:], in_=cst_f[:], pattern=[[1, 128]], compare_op=ALU.is_ge, fill=NEG,
                                       base=0, channel_multiplier=-1), reads=[cB], writes=[cB])
        pool(lambda e: e.tensor_copy(out=negmask_fox[:], in_=cst_f[:]), reads=[cB], writes=[cB])
        pool(lambda e: e.memset(cst_f[:], 0.0), reads=[cB], writes=[cB])
        pool(lambda e: e.affine_select(out=cst_f[:], in_=cst_f[:], pattern=[[1, 128]], compare_op=ALU.is_gt, fill=NEG,
                                       base=0, channel_multiplier=-1), reads=[cB], writes=[cB])
        pool(lambda e: e.tensor_copy(out=negmask_sb[:], in_=cst_f[:]), reads=[cB], writes=[cB])
        pool(lambda e: e.memset(cst_f[:], -1.0), reads=[cB], writes=[cB])
        pool(lambda e: e.affine_select(out=cst_f[:], in_=cst_f[:], pattern=[[-1, 128]], compare_op=ALU.is_ge, fill=0.0,
                                       base=0, channel_multiplier=1), reads=[cB], writes=[cB])
        pool(lambda e: e.tensor_copy(out=negtri[:], in_=cst_f[:]), reads=[cB], writes=[cB])
        pool(lambda e: e.memset(negones[:], -1.0), reads=[cB], writes=[cB])
        pool(lambda e: e.memset(blockones[:], 0.0), reads=[cB], writes=[cB])
        pool(lambda e: e.memset(blockones[0:64, 0:64], 1.0 / 64), reads=[cB], writes=[cB])
        pool(lambda e: e.memset(blockones[64:128, 64:128], 1.0 / 64), reads=[cB], writes=[cB])
        pool(lambda e: e.memset(ones_f[:], 1.0), reads=[cB], writes=[cB])
        pool(lambda e: e.memset(mhalf[:], -0.5), reads=[cB], writes=[cB])
        pool(lambda e: e.memset(cst_eps[:], EPS), reads=[cB], writes=[cB])
        pool(lambda e: e.tensor_scalar(out=qkn[:, 0:1], in0=qkn[:, 0:1], scalar1=0.125, scalar2=None, op0=ALU.mult), reads=[cB2], writes=[cB2])
        pool(lambda e: e.tensor_scalar(out=nfb[:], in0=nfb[:], scalar1=-1.0, scalar2=None, op0=ALU.mult), reads=[cB2], writes=[cB2])
        barrier()

        def norm_stats(tiles):
            n = len(tiles)
            for i, t in enumerate(tiles):
                act(lambda e, t=t, i=i: e.activation(out=sqj[:], in_=xres[:, t, :], func=AF.Square, accum_out=stat[:, i:i + 1]),
                    reads=[xB[t]], writes=[statB[0]])
            act(lambda e, n=n: e.copy(out=stat3[:, 0:n], in_=stat[:, 0:n]), reads=[statB[0]], writes=[statB[0]])
            dve(lambda e, n=n: e.tensor_scalar(out=stat2[:, 0:n], in0=stat3[:, 0:n], scalar1=1.0 / D, scalar2=EPS, op0=ALU.mult, op1=ALU.add),
                reads=[statB[0]], writes=[statB[1]])
            pool(lambda e, n=n: e.tensor_tensor(out=stat2[:, 0:n], in0=stat2[:, 0:n], in1=mhalf[:, 0:n], op=ALU.pow),
                 reads=[statB[1]], writes=[statB[1]])

        def norm_tile(i, t, gi, off=0):
            k = i % 2
            pos = off + i * 128
            dve(lambda e, t=t, k=k, i=i: e.scalar_tensor_tensor(out=xn_t[k][:], in0=xres[:, t, :], scalar=stat2[:, i:i + 1], in1=gbc[:, gi, :],
                                                            op0=ALU.mult, op1=ALU.mult),
                reads=[xB[t], statB[1], gbcB], writes=[xn_B[k]])
            for kc in range(8):
                P.pe_open = 0 if kc == 7 else 1
                P.op("pe", lambda e, k=k, kc=kc: e.transpose(out=trps[:, kc * 128:(kc + 1) * 128], in_=xn_t[k][:, kc * 128:(kc + 1) * 128], identity=ident[:]),
                     reads=[xn_B[k]], writes=[trB], sig=(kc == 7))
            act(lambda e, pos=pos: e.copy(out=xnT[:, :, pos:pos + 128], in_=trps[:].rearrange("p (k c) -> p k c", c=128)),
                reads=[trB], writes=[xnTBs[pos // 512]])

        def norm_T(g, gi, tiles=None, off=0):
            if tiles is None:
                tiles = [4 * g + i for i in range(4)]
            norm_stats(tiles)
            for i, t in enumerate(tiles):
                norm_tile(i, t, gi, off)

        def ffn_phase(li, gi, st, g2s=(0, 1), excl=()):
            wd_sb = sb("wd_sb", [128, NF, 1024], BF16, st)
            wdB = Buf("wd")
            hT = sb("hT", [128, NF, 1024], BF16, st)
            hTB = Buf("hT")
            wgu = [sb("wgu%d" % i, [128, 2, 1024], BF16, st) for i in range(3)]
            wguB = [Buf("wgu%d" % i) for i in range(3)]
            wguD = [get_ds(True) for i in range(3)]
            wguD2 = [get_ds(True) for i in range(3)]
            sil = [sb("sil%d" % i, [128, 512], BF16, st) for i in range(2)]
            silB = [Buf("sil%d" % i) for i in range(2)]
            ds_wd = [get_ds(True) for i in range(11)]
            Dn = [4, 5, 6]
            dcount = 0
            nslot = 0
            chunks = [(g2, f) for g2 in g2s for f in range(NF)]

            def load_chunk(idx):
                g2_, f_ = chunks[idx]
                s_ = idx % 3
                P.dma("pool", wgu[s_][:, 0, :], wg_d[li][f_], wguD[s_], writes=[wguB[s_]])
                P.dma("pool", wgu[s_][:, 1, :], wu_d[li][f_], wguD2[s_], multi=[wguB[s_]])

            tl0 = [8 * g2s[0] + i for i in range(8)]
            norm_stats(tl0)
            for idx in range(3):
                load_chunk(idx)
            for i, t in enumerate(tl0):
                norm_tile(i, t, gi, 0)
            for g2 in g2s:
                if g2 != g2s[0]:
                    norm_T(None, gi, tiles=[8 * g2 + i for i in range(8)])
                for f in range(NF):
                    idx_ = g2s.index(g2) * NF + f
                    s = idx_ % 3
                    for kc in range(8):
                        for h in range(2):
                            hc = slice(h * 512, (h + 1) * 512)
                            mm(banks[h][:, :], wgu[s][:, 0, kc * 128:(kc + 1) * 128], xnT[:, kc, hc], kc == 0, kc == 7, [wguB[s], xnTBs[h]], [bankB[h]], kc == 7)
                    for kc in range(8):
                        for h in range(2):
                            hc = slice(h * 512, (h + 1) * 512)
                            mm(banks[2 + h][:, :], wgu[s][:, 1, kc * 128:(kc + 1) * 128], xnT[:, kc, hc], kc == 0, kc == 7, [wguB[s], xnTBs[h]], [bankB[2 + h]], kc == 7)
                    for h in range(2):
                        hc = slice(h * 512, (h + 1) * 512)
                        act(lambda e, h=h: e.activation(out=sil[h][:], in_=banks[h][:, :], func=AF.Silu), reads=[bankB[h]], writes=[silB[h]])
                        dve(lambda e, h=h, f=f, hc=hc: e.tensor_tensor(out=hT[:, f, hc], in0=banks[2 + h][:, :], in1=sil[h][:], op=ALU.mult),
                            reads=[bankB[2 + h], silB[h]], writes=[hTB])
                    if idx_ + 3 < len(chunks):
                        load_chunk(idx_ + 3)
                    if g2 == g2s[0] and f < 11:
                        P.dma("pool", wd_sb[:, 2 * f:2 * f + 2, :].rearrange("p a n -> p (a n)"), wd_d[li][:, 2 * f * 1024:(2 * f + 2) * 1024], ds_wd[f], multi=[wdB])
                for i in range(8):
                    t = 8 * g2 + i
                    for j in range(2):
                        b = Dn[dcount % 3]
                        dcount += 1
                        for f in range(NF):
                            mm(banks[b][:, :], hT[:, f, i * 128:(i + 1) * 128], wd_sb[:, f, j * 512:(j + 1) * 512], f == 0, f == NF - 1,
                               [hTB, wdB], [bankB[b]], f == NF - 1)
                        dve(lambda e, b=b, t=t, j=j: e.scalar_tensor_tensor(out=xres[:, t, j * 512:(j + 1) * 512], in0=banks[b][:, :], scalar=0.5,
                                                                             in1=xres[:, t, j * 512:(j + 1) * 512], op0=ALU.mult, op1=ALU.add),
                            reads=[bankB[b], xB[t]], writes=[xB[t]])
            barrier(excl=excl)
            put_ds(*ds_wd, *wguD, *wguD2)

        def dump(name):
            if debug:
                ds = get_ds()
                P.dma("sp", dbg[name].rearrange("(i p) d -> p i d", p=128), xres[:], ds, reads=xB)
                barrier()

        def finish():
            barrier()

        ds_cc_sb = get_ds()
        ds_cc_fox = get_ds()
        CCX = (ds_cc_sb, ds_cc_fox)
        XTaB = [Buf("XTa0"), Buf("XTa1")]
        XVaB = Buf("XVa")
        XLaB = Buf("XLa")
        XTB = [[Buf("XT") for h2 in range(2)] for g in range(NG)]
        XVB = [Buf("XV") for g in range(NG)]
        XLB = Buf("XL")
        XLv = XL.ap()

        pending_fox = []

        def a2_phase(st, gs, last):
                wqk_sb = sb("wqk_sb", [128, 16, 1024], BF16, st)
                wv_sb = sb("wv_sb", [128, 2, 4096], BF16, st)
                wfl_sb = sb("wfl_sb", [128, 64], BF16, st)
                wqkB = [Buf("wqk%d" % i) for i in range(4)]
                wvB = [Buf("wv%d" % i) for i in range(2)]
                wflB = Buf("wfl")
                wDs = [get_ds(True) for i in range(7)]
                tl8 = [4 * gs[0] + i for i in range(4)] + [4 * gs[1] + i for i in range(4)]
                norm_stats(tl8)
                for i in range(4):
                    norm_tile(i, tl8[i], 1, 0)
                for c0 in (8, 12):
                    P.dma("pool", wqk_sb[:, c0:c0 + 4, :], wqk_d[c0:c0 + 4].rearrange("c p n -> p c n"), wDs[c0 // 4], writes=[wqkB[c0 // 4]])
                for v in range(2):
                    P.dma("pool", wv_sb[:, v, :], wv_d[v], wDs[4 + v], writes=[wvB[v]])
                for c0 in (0, 4):
                    P.dma("pool", wqk_sb[:, c0:c0 + 4, :], wqk_d[c0:c0 + 4].rearrange("c p n -> p c n"), wDs[c0 // 4], writes=[wqkB[c0 // 4]])
                P.dma("pool", wfl_sb[:], wfl_d, wDs[6], writes=[wflB])
                sq_sb = [sb("sq_sb%d" % i, [128, 512], BF16, st) for i in range(4)]
                sq_B = [Buf("sq_sb%d" % i) for i in range(4)]
                r_sb = [sb("r_sb%d" % i, [128, 512], F32, st) for i in range(4)]
                r_B = [Buf("r_sb%d" % i) for i in range(4)]
                stg = [sb("stg%d" % i, [128, 512], BF16, st) for i in range(NSTG)]
                stgB = [Buf("stg%d" % i) for i in range(NSTG)]
                stgD = [get_ds() for i in range(NSTG)]
                fl_e = sb("fl_e", [8, 512], F32, st)
                fl_sp = sb("fl_sp", [8, 512], F32, st)
                flB = Buf("fl")
                flD = get_ds()
                nstg = 0
                pending = []


                def flush_cc(fox=False):
                    for buf_, obuf_, fn_ in pending:
                        P.dma("pool", None, None, ds_cc_sb, inc=1, reads=[buf_], multi=[obuf_], fn=fn_)
                    del pending[:]

                def emit_v(g, xo, xb):
                    nonlocal nstg
                    for i in range(4):
                        for v in range(2):
                            b = (2 * i + v) % 4
                            for kc in range(8):
                                mm(banks[b][:, :], xnT[:, kc, xo + i * 128:xo + (i + 1) * 128], wv_sb[:, v, kc * 512:(kc + 1) * 512], kc == 0, kc == 7,
                                   [wvB[v], xb], [bankB[b]], kc == 7)
                            s = nstg % NSTG
                            nstg += 1
                            if v == 0:
                                act(lambda e, b=b, s=s: e.copy(out=stg[s][:], in_=banks[b][:, :]), reads=[bankB[b]], writes=[stgB[s]])
                            else:
                                dve(lambda e, b=b, s=s: e.tensor_copy(out=stg[s][:], in_=banks[b][:, :]), reads=[bankB[b]], writes=[stgB[s]])
                            t = 4 * g + i
                            P.dma("sp", XVg[g].ap()[i * 128:(i + 1) * 128, v * 512:(v + 1) * 512], stg[s][:], stgD[s], reads=[stgB[s]], multi=[XVB[g]])
                    pending.append((XVB[g], XVaB, lambda e, g=g: e.collective_compute(
                        "AllGather", ALU.bypass, replica_groups=RG, ins=[XVg[g].ap().opt()], outs=[XVa.ap()[g * 2048:(g + 1) * 2048, :].opt()])))

                def qk_chunks(g, clist, xo, xb):
                    nonlocal nstg
                    for c in clist:
                        typ = c // 4
                        b = c % 4
                        for kc in range(8):
                            mm(banks[b][:, :], wqk_sb[:, c, kc * 128:(kc + 1) * 128], xnT[:, kc, xo:xo + 512], kc == 0, kc == 7, [wqkB[c // 4], xb], [bankB[b]], kc == 7)
                        s = nstg % NSTG
                        nstg += 1
                        if typ < 2:
                            k = c % 4
                            k2 = c % 2
                            act(lambda e, k=k, b=b: e.activation(out=sq_sb[k][:], in_=banks[b][:, :], func=AF.Square), reads=[bankB[b]], writes=[sq_B[k]])
                            mb = 4 + k2
                            mm(banks[mb][:, :], blockones[:], sq_sb[k][:], True, True, [cB, sq_B[k]], [bankB[mb]], True)
                            act(lambda e, k=k, mb=mb: e.activation(out=r_sb[k][:], in_=banks[mb][:, :], func=AF.Ln, bias=cst_eps[:], scale=1.0),
                                reads=[bankB[mb], cB], writes=[r_B[k]])
                            act(lambda e, k=k: e.activation(out=r_sb[k][:], in_=r_sb[k][:], func=AF.Exp, scale=-0.5), reads=[r_B[k]], writes=[r_B[k]])
                            dve(lambda e, k=k, b=b, s=s, typ=typ: e.scalar_tensor_tensor(out=stg[s][:], in0=banks[b][:, :], scalar=qkn[:, typ:typ + 1], in1=r_sb[k][:],
                                                                                       op0=ALU.mult, op1=ALU.mult),
                                reads=[bankB[b], r_B[k], cB], writes=[stgB[s]])
                        elif typ == 2:
                            act(lambda e, b=b, s=s: e.activation(out=stg[s][:], in_=banks[b][:, :], func=AF.Copy, scale=0.125), reads=[bankB[b]], writes=[stgB[s]])
                        else:
                            dve(lambda e, b=b, s=s: e.tensor_copy(out=stg[s][:], in_=banks[b][:, :]), reads=[bankB[b]], writes=[stgB[s]])
                        P.dma("sp", XTg[g][c // 8].ap()[(c % 8) * 128:(c % 8 + 1) * 128, :], stg[s][:], stgD[s], reads=[stgB[s]], multi=[XTB[g][c // 8]])
                        if c % 8 == 7:
                            h2 = c // 8
                            o0 = (h2 * 4 + g) * 4096
                            (pending if h2 == 1 else pending_fox).append((XTB[g][h2], XTaB[h2], lambda e, g=g, h2=h2, o0=o0: e.collective_compute(
                                "AllGather", ALU.bypass, replica_groups=RG, ins=[XTg[g][h2].ap().opt()], outs=[XTa.ap()[o0:o0 + 4096, :].opt()])))

                def fl_part(g, xo, xb):
                    cols = slice(g * 512, (g + 1) * 512)
                    b = 6
                    for kc in range(8):
                        mm(banks[b][0:8, :], wfl_sb[:, kc * 8:(kc + 1) * 8], xnT[:, kc, xo:xo + 512], kc == 0, kc == 7, [wflB, xb], [bankB[b]], kc == 7)
                    act(lambda e, b=b: e.activation(out=fl_e[:], in_=banks[b][0:8, :], func=AF.Exp, bias=nfb[:], scale=-1.0), reads=[bankB[b], cB], writes=[flB])
                    act(lambda e: e.activation(out=fl_sp[:], in_=fl_e[:], func=AF.Ln, bias=1.0, scale=1.0), reads=[flB], writes=[flB])
                    P.dma("sp", XLv[:, cols], fl_sp[:], flD, reads=[flB], multi=[XLB])

                for i in range(4, 8):
                    norm_tile(i, tl8[i], 1, 0)
                for gi_, g in enumerate(gs):
                    qk_chunks(g, range(8, 16), gi_ * 512, xnTBs[gi_])
                    emit_v(g, gi_ * 512, xnTBs[gi_])
                flush_cc()
                for gi_, g in enumerate(gs):
                    qk_chunks(g, range(0, 8), gi_ * 512, xnTBs[gi_])
                    fl_part(g, gi_ * 512, xnTBs[gi_])
                flush_cc(fox=True)
                if last:
                    pending_fox.append((XLB, XLaB, lambda e: e.collective_compute(
                        "AllGather", ALU.bypass, replica_groups=RG, ins=[XL.ap().opt()], outs=[XLa.ap().opt()])))
                barrier(excl=CCX)
                put_ds(flD, *stgD, *wDs)

        with ExitStack() as st:
            ffn_phase(0, 0, st, g2s=(0,), excl=CCX)
        with ExitStack() as st:
            a2_phase(st, (0, 1), False)
        with ExitStack() as st:
            ffn_phase(0, 0, st, g2s=(1,), excl=CCX)
        dump("x1")
        with ExitStack() as st:
            a2_phase(st, (2, 3), True)
        if stop_after == "A2":
            finish()
            return nc

        def pid_r(e):
            if "r" not in _pid:
                _pid["r"] = e.partition_id() % 4
            return _pid["r"]

        ds_cc2 = get_ds()
        with ExitStack() as st:
            QT = [sb("QT%d" % i, [128, SEQ], BF16, st) for i in range(2)]
            KT = [sb("KT%d" % i, [128, SEQ], BF16, st) for i in range(2)]
            Vp = [sb("Vp%d" % i, [128, 65, 65], BF16, st) for i in range(2)]
            hbB = [Buf("headbuf%d" % i) for i in range(2)]
            hbD = [get_ds() for i in range(2)]
            PT = [sb("PT%d" % i, [128, 512], BF16, st) for i in range(3)]
            PTB = [Buf("PT%d" % i) for i in range(3)]
            et = [sb("et%d" % i, [128, 512], F32, st) for i in range(2)]
            etB = [Buf("et%d" % i) for i in range(2)]
            spt = [sb("spt%d" % i, [128, 512], BF16, st) for i in range(4)]
            sptB = [Buf("spt%d" % i) for i in range(4)]
            acc = [sb("acc%d" % i, [128, 512], BF16, st) for i in range(2)]
            accB = [Buf("acc%d" % i) for i in range(2)]
            rec = sb("rec", [128, 512], F32, st)
            recB = Buf("rec")
            bc_sb = rec
            bcB = Buf("bc_sb")
            ystg = [sb("ystg%d" % i, [64, 512], BF16, st) for i in range(2)]
            ystgB = [Buf("ystg%d" % i) for i in range(2)]
            ystgD = [get_ds() for i in range(2)]
            L_sb = sb("L_sb", [128, 2, 64], F32, st)
            Fs = sb("Fs", [128, 2, 64], F32, st)
            tot = sb("tot", [128, 2], F32, st)
            onesr = sb("onesr", [128, 64], F32, st)
            upper_f = sb("upper_f", [128, 128], F32, st)
            pcs = [sb("pcs%d" % i, [128, 2, 64], BF16, st) for i in range(3)]
            fB = Buf("fstuff")
            fD = get_ds()
            FPB = Buf("FP")
            YTB = [Buf("YT%d" % q) for q in range(4)]
            ZB = [0, 1, 2, 3]
            ZS = [0, 1, 2, 3, 6]
            YB = [4, 5]
            BCB = 6

            heads = [("sb", 0), ("sb", 1), ("fox", 0), ("fox", 1)]

            selB = [Buf("sel_fox"), Buf("sel_sb")]
            selD = get_ds()

            def do_select(h2):
                def selqk(e):
                    r = pid_r(e)
                    src = XTa.ap()[h2 * 16384:(h2 + 1) * 16384, :].rearrange("(m row) c -> row m c", row=512)[bass.ds(r * 128, 128), :, :]
                    return e.dma_start(out=MyQK[h2].ap(), in_=src)
                P.dma("sp", None, None, selD, reads=[XTaB[h2]], multi=[selB[h2]], fn=selqk)
                if h2 == 1:
                    def selv(e):
                        r = pid_r(e)
                        src = XVa.ap().rearrange("n (fs col) -> n fs col", fs=2)[:, :, bass.ds(r * 128, 128)]
                        return e.dma_start(out=MyV.ap(), in_=src)
                    P.dma("sp", None, None, selD, reads=[XVaB], multi=[selB[0], selB[1]], fn=selv)
                else:
                    def sell(e):
                        r = pid_r(e)
                        src = XLa.ap().rearrange("(rr h) t -> h rr t", rr=4)[bass.ds(r * 2, 2), :, :]
                        return e.dma_start(out=MyL.ap().rearrange("a (rr t) -> a rr t", rr=4), in_=src)
                    P.dma("sp", None, None, selD, reads=[XLaB], multi=[selB[0]], fn=sell)

            do_select(1)

            def load_head(n):
                kind, a = heads[n]
                hb = n % 2
                h2 = 0 if kind == "fox" else 1
                if kind == "fox":
                    pool(lambda e, hb=hb: e.memset(QT[hb][64:70, :], 1.0), writes=[hbB[hb]])
                    pool(lambda e, hb=hb: e.memset(KT[hb][64:70, :], -1.0), writes=[hbB[hb]])
                first = True
                for g in range(NG):
                    for kq, T in ((0, QT), (1, KT)):
                        src = MyQK[h2].ap()[a * 64:(a + 1) * 64, g * 8 + kq:g * 8 + kq + 7:2, :]
                        dst = T[hb][0:64, :].rearrange("p (rr g c) -> p g rr c", rr=4, g=4)[:, g]
                        if first:
                            P.dma("sp", dst, src, hbD[hb], reads=[selB[h2]], writes=[hbB[hb]])
                            first = False
                        else:
                            P.dma("sp", dst, src, hbD[hb], reads=[selB[h2]], multi=[hbB[hb]])
                    for rr in range(4):
                        n0 = (g * 4 + rr) * 512
                        blk = rr * 16 + g * 4
                        P.dma("sp", Vp[hb][:, blk:blk + 4, 0:64],
                              MyV.ap()[n0:n0 + 512, h2, a * 64:(a + 1) * 64].rearrange("(b p) d -> p b d", p=128),
                              hbD[hb], reads=[selB[h2]], multi=[hbB[hb]])
                if kind == "fox":
                    P.dma("sp", QT[hb][64:67, :], FP.ap()[a], hbD[hb], reads=[FPB], multi=[hbB[hb]])
                    P.dma("sp", KT[hb][67:70, :], FP.ap()[a], hbD[hb], reads=[FPB], multi=[hbB[hb]])

            for hb in range(2):
                pool(lambda e, hb=hb: e.memset(Vp[hb][:, 64, :], 0.0), writes=[hbB[hb]])
                pool(lambda e, hb=hb: e.memset(Vp[hb][:, :, 64:65], 1.0), writes=[hbB[hb]])
                pool(lambda e, hb=hb: e.memset(QT[hb][64:128, :], 0.0), writes=[hbB[hb]])
                pool(lambda e, hb=hb: e.memset(KT[hb][64:128, :], 0.0), writes=[hbB[hb]])
            pool(lambda e: e.memset(onesr[:], 1.0), writes=[fB])
            pool(lambda e: e.memset(upper_f[:], 1.0), writes=[fB])
            pool(lambda e: e.affine_select(out=upper_f[:], in_=upper_f[:], pattern=[[1, 128]], compare_op=ALU.is_gt, fill=0.0,
                                           base=0, channel_multiplier=-1), reads=[fB], writes=[fB])
            load_head(0)
            for buf_, obuf_, fn_ in pending_fox:
                P.dma("pool", None, None, ds_cc_fox, inc=1, reads=[buf_], multi=[obuf_], fn=fn_)
            del pending_fox[:]

            def tiles_of(kind):
                tl = []
                for g in range(16):
                    kbs = list(range(0, 4 * g + 4))
                    if kind == "sb":
                        kbs = kbs[::-1]
                    for idx, kb in enumerate(kbs):
                        j = kb - 4 * g
                        c0 = 128 * j if j >= 0 else 0
                        tl.append(dict(g=g, kb=kb, c0=c0, diag=(j >= 0), first=(idx == 0), last=(idx == len(kbs) - 1)))
                return tl

            ngroup = [0]
            nstg = [0]

            def evac(kind, n, a, t, yb):
                g = t["g"]
                s = nstg[0] % 2
                nstg[0] += 1
                Y = banks[yb]
                if kind == "fox":
                    dve(lambda e, Y=Y: e.reciprocal(out=rec[64:65, :], in_=Y[64:65, :]), reads=[bankB[yb]], writes=[recB])
                    mm(banks[BCB][0:64, :], ones_f[64:65, 0:64], rec[64:65, :], True, True, [cB, recB], [bankB[BCB]], True)
                    act(lambda e: e.copy(out=bc_sb[0:64, :], in_=banks[BCB][0:64, :]), reads=[bankB[BCB]], writes=[bcB])
                    dve(lambda e, Y=Y, s=s: e.tensor_tensor(out=ystg[s][:], in0=Y[0:64, :], in1=bc_sb[0:64, :], op=ALU.mult),
                        reads=[bankB[yb], bcB], writes=[ystgB[s]])
                else:
                    dve(lambda e, Y=Y, s=s: e.tensor_copy(out=ystg[s][:], in_=Y[0:64, :]), reads=[bankB[yb]], writes=[ystgB[s]])
                hi = (0 if kind == "fox" else 2) + a
                q4 = g // 4
                P.dma("sp", YTc[q4].ap()[hi * 64:(hi + 1) * 64, (g % 4) * 512:(g % 4 + 1) * 512], ystg[s][:], ystgD[s],
                      reads=[ystgB[s]], multi=[YTB[q4]])
                if n == len(heads) - 1 and g % 4 == 3:
                    P.dma("pool", None, None, ds_cc2, inc=1, reads=[YTB[q4]],
                          fn=lambda e, q=q4: e.collective_compute("AllGather", ALU.bypass, replica_groups=RG,
                                                               ins=[YTc[q].ap().opt()], outs=[YTa.ap()[q * 1024:(q + 1) * 1024, :].opt()]))

            def run_fox(n, a):
                hb = n % 2
                tl = tiles_of("fox")
                nt = len(tl)
                LA = 2
                gy = {}
                for k in range(nt + LA):
                    if k < nt:
                        t = tl[k]
                        zb = ZB[k % 3]
                        c0 = t["c0"]
                        qc = t["g"] * 512
                        mm(banks[zb][:, c0:512], KT[hb][0:128, t["kb"] * 128:(t["kb"] + 1) * 128], QT[hb][0:128, qc + c0:qc + 512],
                           True, not t["diag"], [hbB[hb]], [bankB[zb]], not t["diag"])
                        if t["diag"]:
                            mm(banks[zb][:, c0:c0 + 128], ident[:], negmask_fox[:], False, True, [cB], [bankB[zb]], True)
                    j = k - LA
                    if j >= 0:
                        t = tl[j]
                        zb = ZB[j % 3]
                        pt = j % 3
                        c0 = t["c0"]
                        if t["first"]:
                            gy[t["g"]] = YB[ngroup[0] % 2]
                            ngroup[0] += 1
                        yb = gy[t["g"]]
                        act(lambda e, zb=zb, pt=pt, c0=c0: e.activation(out=PT[pt][:, c0:512], in_=banks[zb][:, c0:512], func=AF.Exp),
                            reads=[bankB[zb]], writes=[PTB[pt]])
                        mm(banks[yb][:, c0:512], Vp[hb][:].rearrange("p b d -> p (b d)")[:, t["kb"] * 65:t["kb"] * 65 + 128], PT[pt][:, c0:512], t["first"], t["last"],
                           [hbB[hb], PTB[pt]], [bankB[yb]], True)
                        if t["last"]:
                            evac("fox", n, a, t, yb)

            def run_sb(n, a):
                hb = n % 2
                tl = tiles_of("sb")
                nt = len(tl)
                gy = {}
                for i in range(nt + 4):
                    if i < nt:
                        t = tl[i]
                        zb = ZS[i % 5]
                        c0 = t["c0"]
                        qc = t["g"] * 512
                        mm(banks[zb][:, c0:512], KT[hb][0:128, t["kb"] * 128:(t["kb"] + 1) * 128], QT[hb][0:128, qc + c0:qc + 512],
                           True, False, [hbB[hb]], [bankB[zb]], not t["diag"])
                        if t["diag"]:
                            mm(banks[zb][:, c0:c0 + 128], ident[:], negmask_sb[:], False, False, [cB], [bankB[zb]], True)
                        ei = i % 2
                        act(lambda e, zb=zb, ei=ei, c0=c0: e.activation(out=et[ei][:, c0:512], in_=banks[zb][:, c0:512], func=AF.Exp),
                            reads=[bankB[zb]], writes=[etB[ei]])
                    j = i - 1
                    if 0 <= j < nt:
                        ei = j % 2
                        si = j % 4
                        c0 = tl[j]["c0"]
                        act(lambda e, ei=ei, si=si, c0=c0: e.activation(out=spt[si][:, c0:512], in_=et[ei][:, c0:512], func=AF.Ln, bias=1.0, scale=1.0),
                            reads=[etB[ei]], writes=[sptB[si]])
                    j = i - 3
                    if 0 <= j < nt:
                        t = tl[j]
                        zb = ZS[j % 5]
                        si = j % 4
                        c0 = t["c0"]
                        ai = j % 2
                        hasacc = not t["first"]
                        c1 = c0 + 128 if t["diag"] else 0
                        use_acc = hasacc and c1 < 512
                        mm(banks[zb][:, c0:512], negtri[:], spt[si][:, c0:512], False, not use_acc, [cB, sptB[si]], [bankB[zb]], not use_acc)
                        if use_acc:
                            mm(banks[zb][:, c1:512], negones[:], acc[ai][:, c1:512], False, True, [cB, accB[ai]], [bankB[zb]], True)
                        if not t["last"]:
                            if t["diag"]:
                                dve(lambda e, ai=ai, si=si, c0=c0: e.tensor_copy(out=acc[1 - ai][:, c0:c0 + 128], in_=spt[si][:, c0:c0 + 128]),
                                    reads=[sptB[si]], writes=[accB[1 - ai]])
                            if use_acc:
                                dve(lambda e, ai=ai, si=si, c1=c1: e.tensor_tensor(out=acc[1 - ai][:, c1:512], in0=acc[ai][:, c1:512], in1=spt[si][:, c1:512], op=ALU.add),
                                    reads=[accB[ai], sptB[si]], writes=[accB[1 - ai]])
                        pt = j % 3
                        act(lambda e, zb=zb, pt=pt, c0=c0: e.activation(out=PT[pt][:, c0:512], in_=banks[zb][:, c0:512], func=AF.Exp),
                            reads=[bankB[zb]], writes=[PTB[pt]])
                    j = i - 4
                    if 0 <= j < nt:
                        t = tl[j]
                        pt = j % 3
                        c0 = t["c0"]
                        if t["first"]:
                            gy[t["g"]] = YB[ngroup[0] % 2]
                            ngroup[0] += 1
                        yb = gy[t["g"]]
                        mm(banks[yb][:, c0:512], Vp[hb][:].rearrange("p b d -> p (b d)")[:, t["kb"] * 65:t["kb"] * 65 + 128], PT[pt][:, c0:512], t["first"], t["last"],
                           [hbB[hb], PTB[pt]], [bankB[yb]], True)
                        if t["last"]:
                            evac("sb", n, a, t, yb)

            def compute_F():
                P.dma("sp", L_sb[:], MyL.ap().rearrange("a (p j) -> p a j", p=128), fD, reads=[selB[0]], writes=[fB])
                for a_ in range(2):
                    dve(lambda e, a_=a_: e.tensor_tensor_scan(out=Fs[:, a_, :], data0=onesr[:, 0:64], data1=L_sb[:, a_, :], initial=0.0, op0=ALU.mult, op1=ALU.add),
                        reads=[fB], writes=[fB])
                dve(lambda e: e.tensor_copy(out=tot[:], in_=Fs[:, :, 63]), reads=[fB], writes=[fB])
                mm(banks[BCB][:, 0:2], upper_f[:], tot[:], True, True, [fB, cB], [bankB[BCB]], True)
                dve(lambda e: e.tensor_copy(out=tot[:], in_=banks[BCB][:, 0:2]), reads=[bankB[BCB]], writes=[fB])
                for a_ in range(2):
                    dve(lambda e, a_=a_: e.tensor_scalar(out=Fs[:, a_, :], in0=Fs[:, a_, :], scalar1=tot[:, a_:a_ + 1], scalar2=None, op0=ALU.add),
                        reads=[fB], writes=[fB])
                dve(lambda e: e.tensor_copy(out=pcs[0][:], in_=Fs[:]), reads=[fB], writes=[fB])
                dve(lambda e: e.tensor_tensor(out=L_sb[:], in0=Fs[:], in1=pcs[0][:], op=ALU.subtract), reads=[fB], writes=[fB])
                dve(lambda e: e.tensor_copy(out=pcs[1][:], in_=L_sb[:]), reads=[fB], writes=[fB])
                dve(lambda e: e.tensor_tensor(out=Fs[:], in0=L_sb[:], in1=pcs[1][:], op=ALU.subtract), reads=[fB], writes=[fB])
                dve(lambda e: e.tensor_copy(out=pcs[2][:], in_=Fs[:]), reads=[fB], writes=[fB])
                for k3 in range(3):
                    P.dma("sp", FP.ap()[:, k3, :].rearrange("a (p j) -> p a j", p=128), pcs[k3][:], fD, reads=[fB], multi=[FPB])


            nheads = int(os.environ.get("NHEADS", "4"))
            for n in range(nheads):
                kind, a = heads[n]
                if n + 1 < nheads:
                    load_head(n + 1)
                if kind == "fox":
                    run_fox(n, a)
                else:
                    run_sb(n, a)
                if n == 0:
                    do_select(0)
                    compute_F()
            barrier()
            put_ds(fD, selD, *hbD, *ystgD)

        if debug:
            ds = get_ds()
            for q in range(4):
                P.dma("sp", dbg["YT"][:, q * 2048:(q + 1) * 2048], YTc[q].ap(), ds)
            barrier()
        if stop_after == "B":
            finish()
            return nc

        ds_sel = get_ds()

        def sely(e):
            r = pid_r(e)
            return e.dma_start(out=MyY.ap(), in_=YTa.ap()[bass.ds(r * 1024, 1024), :])
        P.dma("sp", None, None, ds_sel, fn=sely)
        barrier()

        with ExitStack() as st:
            ytl = [sb("ytl%d" % i, [128, 8, 512], BF16, st) for i in range(2)]
            ytlB = [Buf("ytl%d" % i) for i in range(2)]
            ytlD = [get_ds() for i in range(2)]
            wbf_sb = sb("wbf_sb", [128, 4, 1024], BF16, st)
            wbs_sb = sb("wbs_sb", [128, 4, 1024], BF16, st)
            wout_sb = sb("wout_sb", [128, 8, 1024], BF16, st)
            wgt_sb = sb("wgt_sb", [128, 16, 1024], BF16, st)
            mT = sb("mT", [128, 8, 512], BF16, st)
            mTB = Buf("mT")
            sg = [sb("sg%d" % i, [128, 512], BF16, st) for i in range(2)]
            sgB = [Buf("sg%d" % i) for i in range(2)]
            tt = [sb("tt%d" % i, [128, 512], F32, st) for i in range(2)]
            ttB = [Buf("tt%d" % i) for i in range(2)]
            wgtB = [Buf("wgt%d" % i) for i in range(4)]
            wbfB, wbsB = Buf("wbf"), Buf("wbs")
            woutB = [Buf("wout%d" % i) for i in range(2)]
            wDs = [get_ds(True) for i in range(8)]
            yv = MyY.ap().rearrange("(ch p) t -> p ch t", p=128)
            norm_T(0, 1)

            def ld_wgt(q):
                P.dma("pool", wgt_sb[:, 4 * q:4 * q + 4, :], wgt_d[4 * q:4 * q + 4].rearrange("c p n -> p c n"), wDs[q], writes=[wgtB[q]])
            ld_wgt(0)
            ld_wgt(2)
            P.dma("pool", wbf_sb[:].rearrange("p a n -> p (a n)"), wbf_d, wDs[4], writes=[wbfB])
            P.dma("pool", wbs_sb[:].rearrange("p a n -> p (a n)"), wbs_d, wDs[5], writes=[wbsB])
            ld_wgt(1)
            ld_wgt(3)
            for h in range(2):
                P.dma("pool", wout_sb[:, 4 * h:4 * h + 4, :].rearrange("p a n -> p (a n)"), wout_d[:, h * 4096:(h + 1) * 4096], wDs[6 + h], writes=[woutB[h]])
            for g in range(NG):
                yb_ = g % 2
                P.dma("sp", ytl[yb_][:], yv[:, :, g * 512:(g + 1) * 512], ytlD[yb_], writes=[ytlB[yb_]])
                xo = (g % 2) * 512
                xb = xnTBs[g % 2]
                nxt = [4 * (g + 1) + i for i in range(4)] if g + 1 < NG else []
                if nxt:
                    norm_stats(nxt)
                for c in range(8):
                    gf, gs = (0, 1) if c % 2 == 0 else (2, 3)
                    zf, zs = 4, 5
                    for kc in range(8):
                        mm(banks[gf][:, :], wgt_sb[:, c, kc * 128:(kc + 1) * 128], xnT[:, kc, xo:xo + 512], kc == 0, kc == 7, [wgtB[c // 4], xb], [bankB[gf]], kc == 7)
                    for kc in range(8):
                        mm(banks[gs][:, :], wgt_sb[:, 8 + c, kc * 128:(kc + 1) * 128], xnT[:, kc, xo:xo + 512], kc == 0, kc == 7, [wgtB[2 + c // 4], xb], [bankB[gs]], kc == 7)
                    for rr in range(4):
                        mm(banks[zf][:, :], wbf_sb[:, rr, c * 128:(c + 1) * 128], ytl[yb_][:, rr * 2, :], rr == 0, rr == 3, [wbfB, ytlB[yb_]], [bankB[zf]], rr == 3)
                    for rr in range(4):
                        mm(banks[zs][:, :], wbs_sb[:, rr, c * 128:(c + 1) * 128], ytl[yb_][:, rr * 2 + 1, :], rr == 0, rr == 3, [wbsB, ytlB[yb_]], [bankB[zs]], rr == 3)
                    act(lambda e, gf=gf: e.activation(out=sg[0][:], in_=banks[gf][:, :], func=AF.Sigmoid), reads=[bankB[gf]], writes=[sgB[0]])
                    act(lambda e, gs=gs: e.activation(out=sg[1][:], in_=banks[gs][:, :], func=AF.Sigmoid), reads=[bankB[gs]], writes=[sgB[1]])
                    dve(lambda e, zf=zf: e.tensor_tensor(out=tt[0][:], in0=banks[zf][:, :], in1=sg[0][:], op=ALU.mult), reads=[bankB[zf], sgB[0]], writes=[ttB[0]])
                    dve(lambda e, zs=zs: e.tensor_tensor(out=tt[1][:], in0=banks[zs][:, :], in1=sg[1][:], op=ALU.mult), reads=[bankB[zs], sgB[1]], writes=[ttB[1]])
                    pool(lambda e, c=c: e.tensor_tensor(out=mT[:, c, :], in0=tt[0][:], in1=tt[1][:], op=ALU.add), reads=[ttB[0], ttB[1]], writes=[mTB])
                    if nxt and c % 2 == 1:
                        norm_tile(c // 2, nxt[c // 2], 1, off=((g + 1) % 2) * 512)
                for i in range(4):
                    t = 4 * g + i
                    for j in range(2):
                        b = 6
                        for c in range(8):
                            mm(banks[b][:, :], mT[:, c, i * 128:(i + 1) * 128], wout_sb[:, c, j * 512:(j + 1) * 512], c == 0, c == 7, [mTB, woutB[c // 4]], [bankB[b]], c == 7)
                        dve(lambda e, b=b, t=t, j=j: e.tensor_tensor(out=xres[:, t, j * 512:(j + 1) * 512], in0=banks[b][:, :], in1=xres[:, t, j * 512:(j + 1) * 512], op=ALU.add),
                            reads=[bankB[b], xB[t]], writes=[xB[t]])
            barrier()
            put_ds(*wDs, *ytlD)
        dump("x2")
        if stop_after == "C1":
            finish()
            return nc

        with ExitStack() as st:
            ffn_phase(1, 2, st)
        dump("x3")

        with ExitStack() as st:
            wpg_sb = sb("wpg_sb", [128, 8, 1024], BF16, st)
            wpp_sb = sb("wpp_sb", [128, 2, 1024], BF16, st)
            wB = Buf("wC3")
            ds_w3 = [get_ds(True) for i in range(3)]
            norm_T(0, 3)
            for h in range(2):
                P.dma("pool", wpg_sb[:, 4 * h:4 * h + 4, :].rearrange("p a n -> p (a n)"), wpg_d[:, h * 4096:(h + 1) * 4096], ds_w3[h], multi=[wB])
            P.dma("pool", wpp_sb[:].rearrange("p a n -> p (a n)"), wpp_d, ds_w3[2], multi=[wB])
            p_b = [sb("p_b%d" % i, [128, 256], BF16, st) for i in range(2)]
            p_B = [Buf("p_b%d" % i) for i in range(2)]
            p_D = [get_ds(True) for i in range(2)]
            pT = sb("pT", [128, 2, 512], BF16, st)
            pTB = Buf("pT")
            sg = [sb("sgp%d" % i, [128, 512], F32, st) for i in range(2)]
            sgB = [Buf("sgp%d" % i) for i in range(2)]
            tt = [sb("ttp%d" % i, [128, 512], F32, st) for i in range(2)]
            ttB = [Buf("ttp%d" % i) for i in range(2)]
            ostg = [sb("ostg%d" % i, [128, 512], F32, st) for i in range(3)]
            ostgB = [Buf("ostg%d" % i) for i in range(3)]
            ostgD = [get_ds() for i in range(3)]
            no = 0
            for g in range(NG):
                xo = (g % 2) * 512
                xb = xnTBs[g % 2]
                nxt = [4 * (g + 1) + i for i in range(4)] if g + 1 < NG else []
                if nxt:
                    norm_stats(nxt)
                for i in range(4):
                    t = 4 * g + i
                    k = i % 2
                    P.dma("pool", p_b[k][:], p_d[t * 128:(t + 1) * 128, :], p_D[k], writes=[p_B[k]])
                    for k2 in range(2):
                        P.pe_open = 0 if k2 == 1 else 1
                        P.op("pe", lambda e, k=k, k2=k2: e.transpose(out=trps[:, k2 * 128:(k2 + 1) * 128], in_=p_b[k][:, k2 * 128:(k2 + 1) * 128], identity=ident[:]),
                             reads=[p_B[k]], writes=[trB], sig=(k2 == 1))
                    act(lambda e, i=i: e.copy(out=pT[:, :, i * 128:(i + 1) * 128], in_=trps[:, 0:256].rearrange("p (k c) -> p k c", c=128)),
                        reads=[trB], writes=[pTB])
                for i in range(4):
                    t = 4 * g + i
                    for j in range(2):
                        u = (2 * i + j) % 2
                        gb, pb = (0, 2) if u == 0 else (1, 3)
                        for kc in range(8):
                            mm(banks[gb][:, :], xnT[:, kc, xo + i * 128:xo + (i + 1) * 128], wpg_sb[:, kc, j * 512:(j + 1) * 512], kc == 0, kc == 7, [xb, wB], [bankB[gb]], kc == 7)
                        for k2 in range(2):
                            mm(banks[pb][:, :], pT[:, k2, i * 128:(i + 1) * 128], wpp_sb[:, k2, j * 512:(j + 1) * 512], k2 == 0, k2 == 1, [pTB, wB], [bankB[pb]], k2 == 1)
                        act(lambda e, gb=gb, u=u: e.activation(out=sg[u][:], in_=banks[gb][:, :], func=AF.Sigmoid), reads=[bankB[gb]], writes=[sgB[u]])
                        dve(lambda e, pb=pb, u=u: e.tensor_tensor(out=tt[u][:], in0=banks[pb][:, :], in1=sg[u][:], op=ALU.mult), reads=[bankB[pb], sgB[u]], writes=[ttB[u]])
                        o = no % 3
                        no += 1
                        pool(lambda e, o=o, u=u, t=t, j=j: e.tensor_tensor(out=ostg[o][:], in0=tt[u][:], in1=xres[:, t, j * 512:(j + 1) * 512], op=ALU.add),
                             reads=[ttB[u], xB[t]], writes=[ostgB[o]])
                        P.dma("sp", out_d[t * 128:(t + 1) * 128, j * 512:(j + 1) * 512], ostg[o][:], ostgD[o], reads=[ostgB[o]])
                        if nxt and j == 1:
                            norm_tile(i, nxt[i], 3, off=((g + 1) % 2) * 512)
            barrier()
        finish()
    return nc


def _tile_cols(w, c0, ncols, cw):
    sub = w[:, c0:c0 + ncols]
    k = sub.shape[0] // 128
    t = sub.reshape(k, 128, ncols // cw, cw)
    return np.ascontiguousarray(t.transpose(2, 1, 0, 3).reshape(ncols // cw, 128, k * cw))


def _tile_rows(w):
    r = w.shape[0] // 128
    return np.ascontiguousarray(w.reshape(r, 128, w.shape[1]).transpose(1, 0, 2).reshape(128, r * w.shape[1]))


def prep_inputs(inp):
    f = lambda a: np.asarray(a, dtype=np.float32)
    x = f(inp["x"]).reshape(8, TOK, D)
    p = f(inp["p"])[0].reshape(8, TOK, 256)
    w_in = f(inp["w_in"])[0]
    shared = {
        "wg1": _tile_cols(f(inp["ffn1_w_gate"])[0], 0, DFF, 128),
        "wu1": _tile_cols(f(inp["ffn1_w_up"])[0], 0, DFF, 128),
        "wd1": _tile_rows(f(inp["ffn1_w_down"])[0]),
        "wg2": _tile_cols(f(inp["ffn2_w_gate"])[0], 0, DFF, 128),
        "wu2": _tile_cols(f(inp["ffn2_w_up"])[0], 0, DFF, 128),
        "wd2": _tile_rows(f(inp["ffn2_w_down"])[0]),
        "wqk": np.concatenate([_tile_cols(w_in, C_FQ, 512, 128), _tile_cols(w_in, C_FK, 512, 128),
                               _tile_cols(w_in, C_SQ, 512, 128), _tile_cols(w_in, C_SK, 512, 128)], 0),
        "wv": np.concatenate([_tile_cols(w_in, C_FV, 512, 512), _tile_cols(w_in, C_SV, 512, 512)], 0),
        "wfl": _tile_cols(w_in, C_FL, 8, 8)[0],
        "wgt": np.concatenate([_tile_cols(w_in, C_GF, 1024, 128), _tile_cols(w_in, C_GS, 1024, 128)], 0),
        "wbf": _tile_rows(f(inp["w_branch_fox"])[0]),
        "wbs": _tile_rows(f(inp["w_branch_sb"])[0]),
        "wout": _tile_rows(f(inp["w_out"])[0]),
        "wpg": _tile_rows(f(inp["w_ple_gate"])[0]),
        "wpp": _tile_rows(f(inp["w_ple_proj"])[0]),
        "gains": np.ascontiguousarray(np.stack([f(inp["ffn1_norm"])[0], f(inp["mix_norm"])[0], f(inp["ffn2_norm"])[0], f(inp["ple_norm"])[0]], 0)),
        "fb": np.ascontiguousarray(f(inp["forget_bias"])[0].reshape(8, 1)),
        "qkn": np.ascontiguousarray(np.stack([np.tile(f(inp["q_norm"])[0], 2), np.tile(f(inp["k_norm"])[0], 2)], 1)),
    }
    maps = []
    for c in range(8):
        m = dict(shared)
        m["x"] = np.ascontiguousarray(x[c])
        m["p"] = np.ascontiguousarray(p[c])
        maps.append(m)
    return maps


_NC_CACHE = {}


def kernel(**inputs):
    if "nc" not in _NC_CACHE:
        _NC_CACHE["nc"] = build_program()
    nc = _NC_CACHE["nc"]
    maps = prep_inputs(inputs)
    res = run_bass_kernel_spmd(nc, maps, core_ids=list(range(8)))
    out = np.stack([np.asarray(r["out"], dtype=np.float32) for r in res.results], 0)
    return out.reshape(2, SEQ, D)
```

```python
import os
import numpy as np
import concourse.bass as bass
import concourse.mybir as mybir
from concourse.bass_utils import run_bass_kernel_spmd

F32 = mybir.dt.float32
BF16 = mybir.dt.bfloat16
AF = mybir.ActivationFunctionType
ALU = mybir.AluOpType
AX = mybir.AxisListType


class Buf:
    __slots__ = ("name", "w", "r", "ws")

    def __init__(self, name=""):
        self.name = name
        self.w = None
        self.r = {}
        self.ws = {}


class DmaSem:
    __slots__ = ("sem", "n", "sw")

    def __init__(self, sem):
        self.sem = sem
        self.n = 0
        self.sw = False


class Prog:
    ENGS = ("pe", "act", "dve", "pool", "sp")

    def __init__(self, nc, engsem):
        self.nc = nc
        self.engsem = engsem
        self.lists = {e: [] for e in self.ENGS}
        self.count = {e: 0 for e in self.ENGS}
        self.seen = {e: {} for e in self.ENGS}
        self.nwaits = 0

    def _wait(self, eng, tok):
        kind, key, val = tok
        if kind == "e" and key == eng and eng == "pe":
            return
        seen = self.seen[eng]
        if seen.get(key, 0) >= val:
            return
        seen[key] = val
        sem = self.engsem[key] if kind == "e" else key.sem
        self.nwaits += 1
        self.lists[eng].append(lambda e, sem=sem, val=val: e.wait_ge(sem, val))

    def _deps(self, eng, reads, writes):
        need = {}

        def add(tok):
            kind, key, val = tok
            if need.get(key, (None, None, 0))[2] < val:
                need[key] = tok

        for b in reads:
            if b.w is not None:
                add(b.w)
            for t in b.ws.values():
                add(t)
        for b in writes:
            if b.w is not None:
                add(b.w)
            for t in b.ws.values():
                add(t)
            for t in b.r.values():
                add(t)
        for tok in need.values():
            self._wait(eng, tok)

    def op(self, eng, fn, reads=(), writes=(), sig=True):
        self._deps(eng, reads, writes)
        if sig:
            self.count[eng] += 1
            tok = ("e", eng, self.count[eng])
            sem = self.engsem[eng]
            self.lists[eng].append(lambda e, fn=fn, sem=sem: fn(e).then_inc(sem, 1))
        else:
            tok = ("e", eng, self.count[eng] + 1)
            self.lists[eng].append(fn)
        for b in writes:
            b.w = tok
            b.r = {}
            b.ws = {}
        for b in reads:
            b.r[eng] = tok
        return tok

    def dma(self, q, out, in_, ds, reads=(), writes=(), inc=16, fn=None, multi=()):
        self._deps(q, reads, writes)
        for b in multi:
            for t in b.r.values():
                self._wait(q, t)
        ds.n += inc
        tok = ("d", ds, ds.n)
        sem = ds.sem
        if fn is None:
            fn = lambda e: e.dma_start(out=out, in_=in_)
        self.lists[q].append(lambda e, fn=fn, sem=sem, inc=inc: fn(e).then_inc(sem, inc))
        for b in writes:
            b.w = tok
            b.r = {}
            b.ws = {}
        for b in multi:
            b.ws[ds] = tok
        for b in reads:
            b.r[ds] = tok
        return tok

    def wait_tok(self, eng, tok):
        self._wait(eng, tok)

    def emit(self, block):
        L = self.lists

        @block.tensor
        def _(e):
            for f in L["pe"]:
                f(e)

        @block.scalar
        def _(e):
            for f in L["act"]:
                f(e)

        @block.vector
        def _(e):
            for f in L["dve"]:
                f(e)

        @block.gpsimd
        def _(e):
            for f in L["pool"]:
                f(e)

        @block.sync
        def _(e):
            for f in L["sp"]:
                f(e)

    def barrier(self, bar_sem, all_ds, excl=()):
        assert self.pe_open == 0
        all_ds = [d_ for d_ in all_ds if d_ not in excl]
        for e in self.ENGS:
            if e != "sp" and self.count[e] > 0:
                self._wait("sp", ("e", e, self.count[e]))
        for ds in all_ds:
            if ds.n > 0:
                self._wait("sp", ("d", ds, ds.n))
        self.nbar += 1
        n = self.nbar
        self.lists["sp"].append(lambda e, n=n: e.sem_inc(bar_sem, 1))
        for e in self.ENGS:
            if e == "sp":
                continue
            self.lists[e].append(lambda eng, n=n: eng.wait_ge(bar_sem, n))
        for e in self.ENGS:
            for e2 in self.ENGS:
                self.seen[e][e2] = self.count[e2]
            for ds in all_ds:
                self.seen[e][ds] = ds.n

    pe_open = 0
    nbar = 0


D = 1024
DFF = 2816
NF = 22
TOK = 2048
NT = 16
SEQ = 8192
NG = 4
EPS = 1e-6
C_FQ, C_FK, C_FV, C_FL, C_SQ, C_SK, C_SV, C_GF, C_GS = 0, 512, 1024, 1536, 1544, 2056, 2568, 3080, 4104
NEG = -30000.0
NSTG = 8
RG = [[0, 1, 2, 3], [4, 5, 6, 7]]


def build_program(debug=False, stop_after=None):
    from contextlib import ExitStack
    nc = bass.Bass("TRN2", target_bir_lowering=False)

    def din(name, shape, dt=F32):
        return nc.dram_tensor(name, list(shape), dt, kind="ExternalInput").ap()

    x_d = din("x", [TOK, D])
    p_d = din("p", [TOK, 256])
    wg_d = [din("wg1", [NF, 128, 1024]), din("wg2", [NF, 128, 1024])]
    wu_d = [din("wu1", [NF, 128, 1024]), din("wu2", [NF, 128, 1024])]
    wd_d = [din("wd1", [128, NF * 1024]), din("wd2", [128, NF * 1024])]
    wqk_d = din("wqk", [16, 128, 1024])
    wv_d = din("wv", [2, 128, 8 * 512])
    wfl_d = din("wfl", [128, 64])
    wgt_d = din("wgt", [16, 128, 1024])
    wbf_d = din("wbf", [128, 4 * 1024])
    wbs_d = din("wbs", [128, 4 * 1024])
    wout_d = din("wout", [128, 8 * 1024])
    wpg_d = din("wpg", [128, 8 * 1024])
    wpp_d = din("wpp", [128, 2 * 1024])
    gains_d = din("gains", [4, D])
    fb_d = din("fb", [8, 1])
    qkn_d = din("qkn", [128, 2])
    out_d = nc.dram_tensor("out", [TOK, D], F32, kind="ExternalOutput").ap()

    XTg = [[nc.dram_tensor("XT_%d_%d" % (g, h2), [1024, 512], BF16) for h2 in range(2)] for g in range(NG)]
    XVg = [nc.dram_tensor("XV_%d" % g, [512, 1024], BF16) for g in range(NG)]
    XL = nc.dram_tensor("XL", [8, TOK], F32)
    XTa = nc.dram_tensor("XTa", [8 * 4096, 512], BF16)
    XVa = nc.dram_tensor("XVa", [4 * 2048, 1024], BF16)
    XLa = nc.dram_tensor("XLa", [32, TOK], F32)
    YTc = [nc.dram_tensor("YT_%d" % q, [256, 2048], BF16) for q in range(4)]
    YTa = nc.dram_tensor("YTa", [4 * 1024, 2048], BF16)
    FP = nc.dram_tensor("FP", [2, 3, SEQ], BF16)
    MyQK = [nc.dram_tensor("MyQK%d" % h2, [128, 32, 512], BF16) for h2 in range(2)]
    MyV = nc.dram_tensor("MyV", [SEQ, 2, 128], BF16)
    MyL = nc.dram_tensor("MyL", [2, SEQ], F32)
    MyY = nc.dram_tensor("MyY", [1024, 2048], BF16)
    dbg = {}
    if debug:
        dbg["XT"] = nc.dram_tensor("dbg_XT", [2048, TOK], BF16, kind="ExternalOutput").ap()
        dbg["XV"] = nc.dram_tensor("dbg_XV", [TOK, 1024], BF16, kind="ExternalOutput").ap()
        dbg["XL"] = nc.dram_tensor("dbg_XL", [8, TOK], F32, kind="ExternalOutput").ap()
        dbg["YT"] = nc.dram_tensor("dbg_YT", [256, SEQ], BF16, kind="ExternalOutput").ap()
        dbg["x1"] = nc.dram_tensor("dbg_x1", [TOK, D], F32, kind="ExternalOutput").ap()
        dbg["x2"] = nc.dram_tensor("dbg_x2", [TOK, D], F32, kind="ExternalOutput").ap()
        dbg["x3"] = nc.dram_tensor("dbg_x3", [TOK, D], F32, kind="ExternalOutput").ap()

    es = ExitStack()
    with es:
        _uid = [0]

        def sb(name, shape, dt, stack=es):
            _uid[0] += 1
            if os.environ.get("SBDBG"):
                print("alloc", name, shape, dt, "remaining", nc.sbuf_bytes_remaining)
            return stack.enter_context(nc.sbuf_tensor("s%d_%s" % (_uid[0], name), list(shape), dt))

        def ps(name, shape, dt):
            return es.enter_context(nc.psum_tensor(name, list(shape), dt))

        sem_pool = [es.enter_context(nc.semaphore("sm%d" % i)) for i in range(94)]
        engsem = {e: sem_pool.pop() for e in Prog.ENGS}
        bar_sem = sem_pool.pop()
        P = Prog(nc, engsem)
        all_ds = []
        free_ds = []

        free_ds_sw = []

        def get_ds(sw=False):
            fl_ = free_ds_sw if sw else free_ds
            if fl_:
                return fl_.pop()
            ds = DmaSem(sem_pool.pop())
            ds.sw = sw
            all_ds.append(ds)
            return ds

        def put_ds(*dss):
            for d_ in dss:
                (free_ds_sw if d_.sw else free_ds).append(d_)

        _pid = {}

        def barrier(flush=True, excl=()):
            P.barrier(bar_sem, all_ds, excl)
            if flush:
                with nc.Block() as block:
                    P.emit(block)
                for k_ in P.lists:
                    P.lists[k_] = []
                _pid.clear()

        banks = [ps("bank%d" % i, [128, 512], F32) for i in range(7)]
        bankB = [Buf("bank%d" % i) for i in range(7)]
        trps = ps("trps", [128, 1024], BF16)
        trB = Buf("trps")

        xres = sb("xres", [128, NT, D], F32)
        xB = [Buf("x%d" % i) for i in range(NT)]
        gbc = sb("gbc", [128, 4, D], F32)
        gbcB = Buf("gbc")
        ident = sb("ident", [128, 128], BF16)
        cst_f = sb("cst_f", [128, 128], F32)
        ones_f = sb("ones_f", [128, 64], F32)
        negmask_fox = sb("nm_fox", [128, 128], BF16)
        negmask_sb = sb("nm_sb", [128, 128], BF16)
        negtri = sb("negtri", [128, 128], BF16)
        negones = sb("negones", [128, 128], BF16)
        blockones = sb("blockones", [128, 128], BF16)
        mhalf = sb("mhalf", [128, 8], F32)
        cst_eps = sb("cst_eps", [128, 1], F32)
        qkn = sb("qkn", [128, 2], F32)
        nfb = sb("nfb", [8, 1], F32)
        cB = Buf("consts")
        xnT = sb("xnT", [128, 8, 1024], BF16)
        xnTBs = [Buf("xnT0"), Buf("xnT1")]
        xn_t = [sb("xn%d" % i, [128, D], BF16) for i in range(2)]
        xn_B = [Buf("xn%d" % i) for i in range(2)]
        sqj = sb("sqj", [128, D], BF16)
        sqjB = Buf("sqj")
        stat = sb("stat", [128, 8], F32)
        statB = [Buf("stat%d" % i) for i in range(2)]
        stat2 = sb("stat2", [128, 8], F32)
        stat3 = sb("stat3", [128, 8], F32)

        ds_c = get_ds()
        def pool(fn, reads=(), writes=()):
            return P.op("pool", fn, reads, writes)

        def dve(fn, reads=(), writes=()):
            return P.op("dve", fn, reads, writes)

        def act(fn, reads=(), writes=()):
            return P.op("act", fn, reads, writes)

        def mm(out, lhsT, rhs, start, stop, reads, writes, sig):
            if sig:
                P.pe_open = 0
            else:
                P.pe_open = 1
            return P.op("pe", lambda e: e.matmul(out, lhsT=lhsT, rhs=rhs, start=start, stop=stop), reads, writes, sig=sig)

        ds_x = get_ds()
        xv = x_d.rearrange("(i p) d -> p i d", p=128)
        for q in range(4):
            P.dma("sp", xres[:, 4 * q:4 * q + 4, :], xv[:, 4 * q:4 * q + 4, :], ds_x, writes=xB[4 * q:4 * q + 4])
        P.dma("sp", gbc[:].rearrange("p a d -> p (a d)"), gains_d.rearrange("a d -> (a d)").partition_broadcast(128), ds_c, writes=[gbcB])
        cB2 = Buf("consts_dma")
        ds_c2 = get_ds()
        P.dma("sp", qkn[:], qkn_d, ds_c2, writes=[cB2])
        P.dma("sp", nfb[:], fb_d, ds_c2, multi=[cB2])

        pool(lambda e: e.memset(cst_f[:], 0.0), writes=[cB])
        pool(lambda e: e.affine_select(out=cst_f[:], in_=cst_f[:], pattern=[[-1, 128]], compare_op=ALU.not_equal, fill=1.0,
                                       base=0, channel_multiplier=1), reads=[cB], writes=[cB])
        pool(lambda e: e.tensor_copy(out=ident[:], in_=cst_f[:]), reads=[cB], writes=[cB])
        pool(lambda e: e.memset(cst_f[:], 0.0), reads=[cB], writes=[cB])
        pool(lambda e: e.affine_select(out=cst_f[:], in_=cst_f[:], pattern=[[1, 128]], compare_op=ALU.is_ge, fill=NEG,
                                       base=0, channel_multiplier=-1), reads=[cB], writes=[cB])
        pool(lambda e: e.tensor_copy(out=negmask_fox[:], in_=cst_f[:]), reads=[cB], writes=[cB])
        pool(lambda e: e.memset(cst_f[:], 0.0), reads=[cB], writes=[cB])
        pool(lambda e: e.affine_select(out=cst_f[:], in_=cst_f[:], pattern=[[1, 128]], compare_op=ALU.is_gt, fill=NEG,
                                       base=0, channel_multiplier=-1), reads=[cB], writes=[cB])
        pool(lambda e: e.tensor_copy(out=negmask_sb[:], in_=cst_f[:]), reads=[cB], writes=[cB])
        pool(lambda e: e.memset(cst_f[:], -1.0), reads=[cB], writes=[cB])
        pool(lambda e: e.affine_select(out=cst_f[:], in_=cst_f[:], pattern=[[-1, 128]], compare_op=ALU.is_ge, fill=0.0,
                                       base=0, channel_multiplier=1), reads=[cB], writes=[cB])
        pool(lambda e: e.tensor_copy(out=negtri[:], in_=cst_f[:]), reads=[cB], writes=[cB])
        pool(lambda e: e.memset(negones[:], -1.0), reads=[cB], writes=[cB])
        pool(lambda e: e.memset(blockones[:], 0.0), reads=[cB], writes=[cB])
        pool(lambda e: e.memset(blockones[0:64, 0:64], 1.0 / 64), reads=[cB], writes=[cB])
        pool(lambda e: e.memset(blockones[64:128, 64:128], 1.0 / 64), reads=[cB], writes=[cB])
        pool(lambda e: e.memset(ones_f[:], 1.0), reads=[cB], writes=[cB])
        pool(lambda e: e.memset(mhalf[:], -0.5), reads=[cB], writes=[cB])
        pool(lambda e: e.memset(cst_eps[:], EPS), reads=[cB], writes=[cB])
        pool(lambda e: e.tensor_scalar(out=qkn[:, 0:1], in0=qkn[:, 0:1], scalar1=0.125, scalar2=None, op0=ALU.mult), reads=[cB2], writes=[cB2])
        pool(lambda e: e.tensor_scalar(out=nfb[:], in0=nfb[:], scalar1=-1.0, scalar2=None, op0=ALU.mult), reads=[cB2], writes=[cB2])
        barrier()

        def norm_stats(tiles):
            n = len(tiles)
            for i, t in enumerate(tiles):
                act(lambda e, t=t, i=i: e.activation(out=sqj[:], in_=xres[:, t, :], func=AF.Square, accum_out=stat[:, i:i + 1]),
                    reads=[xB[t]], writes=[statB[0]])
            act(lambda e, n=n: e.copy(out=stat3[:, 0:n], in_=stat[:, 0:n]), reads=[statB[0]], writes=[statB[0]])
            dve(lambda e, n=n: e.tensor_scalar(out=stat2[:, 0:n], in0=stat3[:, 0:n], scalar1=1.0 / D, scalar2=EPS, op0=ALU.mult, op1=ALU.add),
                reads=[statB[0]], writes=[statB[1]])
            pool(lambda e, n=n: e.tensor_tensor(out=stat2[:, 0:n], in0=stat2[:, 0:n], in1=mhalf[:, 0:n], op=ALU.pow),
                 reads=[statB[1]], writes=[statB[1]])

        def norm_tile(i, t, gi, off=0):
            k = i % 2
            pos = off + i * 128
            dve(lambda e, t=t, k=k, i=i: e.scalar_tensor_tensor(out=xn_t[k][:], in0=xres[:, t, :], scalar=stat2[:, i:i + 1], in1=gbc[:, gi, :],
                                                            op0=ALU.mult, op1=ALU.mult),
                reads=[xB[t], statB[1], gbcB], writes=[xn_B[k]])
            for kc in range(8):
                P.pe_open = 0 if kc == 7 else 1
                P.op("pe", lambda e, k=k, kc=kc: e.transpose(out=trps[:, kc * 128:(kc + 1) * 128], in_=xn_t[k][:, kc * 128:(kc + 1) * 128], identity=ident[:]),
                     reads=[xn_B[k]], writes=[trB], sig=(kc == 7))
            act(lambda e, pos=pos: e.copy(out=xnT[:, :, pos:pos + 128], in_=trps[:].rearrange("p (k c) -> p k c", c=128)),
                reads=[trB], writes=[xnTBs[pos // 512]])

        def norm_T(g, gi, tiles=None, off=0):
            if tiles is None:
                tiles = [4 * g + i for i in range(4)]
            norm_stats(tiles)
            for i, t in enumerate(tiles):
                norm_tile(i, t, gi, off)

        def ffn_phase(li, gi, st, g2s=(0, 1), excl=()):
            wd_sb = sb("wd_sb", [128, NF, 1024], BF16, st)
            wdB = Buf("wd")
            hT = sb("hT", [128, NF, 1024], BF16, st)
            hTB = Buf("hT")
            wgu = [sb("wgu%d" % i, [128, 2, 1024], BF16, st) for i in range(3)]
            wguB = [Buf("wgu%d" % i) for i in range(3)]
            wguD = [get_ds(True) for i in range(3)]
            wguD2 = [get_ds(True) for i in range(3)]
            sil = [sb("sil%d" % i, [128, 512], BF16, st) for i in range(2)]
            silB = [Buf("sil%d" % i) for i in range(2)]
            ds_wd = [get_ds(True) for i in range(11)]
            Dn = [4, 5, 6]
            dcount = 0
            nslot = 0
            chunks = [(g2, f) for g2 in g2s for f in range(NF)]

            def load_chunk(idx):
                g2_, f_ = chunks[idx]
                s_ = idx % 3
                P.dma("pool", wgu[s_][:, 0, :], wg_d[li][f_], wguD[s_], writes=[wguB[s_]])
                P.dma("pool", wgu[s_][:, 1, :], wu_d[li][f_], wguD2[s_], multi=[wguB[s_]])

            tl0 = [8 * g2s[0] + i for i in range(8)]
            norm_stats(tl0)
            for idx in range(3):
                load_chunk(idx)
            for i, t in enumerate(tl0):
                norm_tile(i, t, gi, 0)
            for g2 in g2s:
                if g2 != g2s[0]:
                    norm_T(None, gi, tiles=[8 * g2 + i for i in range(8)])
                for f in range(NF):
                    idx_ = g2s.index(g2) * NF + f
                    s = idx_ % 3
                    for kc in range(8):
                        for h in range(2):
                            hc = slice(h * 512, (h + 1) * 512)
                            mm(banks[h][:, :], wgu[s][:, 0, kc * 128:(kc + 1) * 128], xnT[:, kc, hc], kc == 0, kc == 7, [wguB[s], xnTBs[h]], [bankB[h]], kc == 7)
                    for kc in range(8):
                        for h in range(2):
                            hc = slice(h * 512, (h + 1) * 512)
                            mm(banks[2 + h][:, :], wgu[s][:, 1, kc * 128:(kc + 1) * 128], xnT[:, kc, hc], kc == 0, kc == 7, [wguB[s], xnTBs[h]], [bankB[2 + h]], kc == 7)
                    for h in range(2):
                        hc = slice(h * 512, (h + 1) * 512)
                        act(lambda e, h=h: e.activation(out=sil[h][:], in_=banks[h][:, :], func=AF.Silu), reads=[bankB[h]], writes=[silB[h]])
                        dve(lambda e, h=h, f=f, hc=hc: e.tensor_tensor(out=hT[:, f, hc], in0=banks[2 + h][:, :], in1=sil[h][:], op=ALU.mult),
                            reads=[bankB[2 + h], silB[h]], writes=[hTB])
                    if idx_ + 3 < len(chunks):
                        load_chunk(idx_ + 3)
                    if g2 == g2s[0] and f < 11:
                        P.dma("pool", wd_sb[:, 2 * f:2 * f + 2, :].rearrange("p a n -> p (a n)"), wd_d[li][:, 2 * f * 1024:(2 * f + 2) * 1024], ds_wd[f], multi=[wdB])
                for i in range(8):
                    t = 8 * g2 + i
                    for j in range(2):
                        b = Dn[dcount % 3]
                        dcount += 1
                        for f in range(NF):
                            mm(banks[b][:, :], hT[:, f, i * 128:(i + 1) * 128], wd_sb[:, f, j * 512:(j + 1) * 512], f == 0, f == NF - 1,
                               [hTB, wdB], [bankB[b]], f == NF - 1)
                        dve(lambda e, b=b, t=t, j=j: e.scalar_tensor_tensor(out=xres[:, t, j * 512:(j + 1) * 512], in0=banks[b][:, :], scalar=0.5,
                                                                             in1=xres[:, t, j * 512:(j + 1) * 512], op0=ALU.mult, op1=ALU.add),
                            reads=[bankB[b], xB[t]], writes=[xB[t]])
            barrier(excl=excl)
            put_ds(*ds_wd, *wguD, *wguD2)

        def dump(name):
            if debug:
                ds = get_ds()
                P.dma("sp", dbg[name].rearrange("(i p) d -> p i d", p=128), xres[:], ds, reads=xB)
                barrier()

        def finish():
            barrier()

        ds_cc_sb = get_ds()
        ds_cc_fox = get_ds()
        CCX = (ds_cc_sb, ds_cc_fox)
        XTaB = [Buf("XTa0"), Buf("XTa1")]
        XVaB = Buf("XVa")
        XLaB = Buf("XLa")
        XTB = [[Buf("XT") for h2 in range(2)] for g in range(NG)]
        XVB = [Buf("XV") for g in range(NG)]
        XLB = Buf("XL")
        XLv = XL.ap()

        pending_fox = []

        def a2_phase(st, gs, last):
                wqk_sb = sb("wqk_sb", [128, 16, 1024], BF16, st)
                wv_sb = sb("wv_sb", [128, 2, 4096], BF16, st)
                wfl_sb = sb("wfl_sb", [128, 64], BF16, st)
                wqkB = [Buf("wqk%d" % i) for i in range(4)]
                wvB = [Buf("wv%d" % i) for i in range(2)]
                wflB = Buf("wfl")
                wDs = [get_ds(True) for i in range(7)]
                tl8 = [4 * gs[0] + i for i in range(4)] + [4 * gs[1] + i for i in range(4)]
                norm_stats(tl8)
                for i in range(4):
                    norm_tile(i, tl8[i], 1, 0)
                for c0 in (8, 12):
                    P.dma("pool", wqk_sb[:, c0:c0 + 4, :], wqk_d[c0:c0 + 4].rearrange("c p n -> p c n"), wDs[c0 // 4], writes=[wqkB[c0 // 4]])
                for v in range(2):
                    P.dma("pool", wv_sb[:, v, :], wv_d[v], wDs[4 + v], writes=[wvB[v]])
                for c0 in (0, 4):
                    P.dma("pool", wqk_sb[:, c0:c0 + 4, :], wqk_d[c0:c0 + 4].rearrange("c p n -> p c n"), wDs[c0 // 4], writes=[wqkB[c0 // 4]])
                P.dma("pool", wfl_sb[:], wfl_d, wDs[6], writes=[wflB])
                sq_sb = [sb("sq_sb%d" % i, [128, 512], BF16, st) for i in range(4)]
                sq_B = [Buf("sq_sb%d" % i) for i in range(4)]
                r_sb = [sb("r_sb%d" % i, [128, 512], F32, st) for i in range(4)]
                r_B = [Buf("r_sb%d" % i) for i in range(4)]
                stg = [sb("stg%d" % i, [128, 512], BF16, st) for i in range(NSTG)]
                stgB = [Buf("stg%d" % i) for i in range(NSTG)]
                stgD = [get_ds() for i in range(NSTG)]
                fl_e = sb("fl_e", [8, 512], F32, st)
                fl_sp = sb("fl_sp", [8, 512], F32, st)
                flB = Buf("fl")
                flD = get_ds()
                nstg = 0
                pending = []


                def flush_cc(fox=False):
                    for buf_, obuf_, fn_ in pending:
                        P.dma("pool", None, None, ds_cc_sb, inc=1, reads=[buf_], multi=[obuf_], fn=fn_)
                    del pending[:]

                def emit_v(g, xo, xb):
                    nonlocal nstg
                    for i in range(4):
                        for v in range(2):
                            b = (2 * i + v) % 4
                            for kc in range(8):
                                mm(banks[b][:, :], xnT[:, kc, xo + i * 128:xo + (i + 1) * 128], wv_sb[:, v, kc * 512:(kc + 1) * 512], kc == 0, kc == 7,
                                   [wvB[v], xb], [bankB[b]], kc == 7)
                            s = nstg % NSTG
                            nstg += 1
                            if v == 0:
                                act(lambda e, b=b, s=s: e.copy(out=stg[s][:], in_=banks[b][:, :]), reads=[bankB[b]], writes=[stgB[s]])
                            else:
                                dve(lambda e, b=b, s=s: e.tensor_copy(out=stg[s][:], in_=banks[b][:, :]), reads=[bankB[b]], writes=[stgB[s]])
                            t = 4 * g + i
                            P.dma("sp", XVg[g].ap()[i * 128:(i + 1) * 128, v * 512:(v + 1) * 512], stg[s][:], stgD[s], reads=[stgB[s]], multi=[XVB[g]])
                    pending.append((XVB[g], XVaB, lambda e, g=g: e.collective_compute(
                        "AllGather", ALU.bypass, replica_groups=RG, ins=[XVg[g].ap().opt()], outs=[XVa.ap()[g * 2048:(g + 1) * 2048, :].opt()])))

                def qk_chunks(g, clist, xo, xb):
                    nonlocal nstg
                    for c in clist:
                        typ = c // 4
                        b = c % 4
                        for kc in range(8):
                            mm(banks[b][:, :], wqk_sb[:, c, kc * 128:(kc + 1) * 128], xnT[:, kc, xo:xo + 512], kc == 0, kc == 7, [wqkB[c // 4], xb], [bankB[b]], kc == 7)
                        s = nstg % NSTG
                        nstg += 1
                        if typ < 2:
                            k = c % 4
                            k2 = c % 2
                            act(lambda e, k=k, b=b: e.activation(out=sq_sb[k][:], in_=banks[b][:, :], func=AF.Square), reads=[bankB[b]], writes=[sq_B[k]])
                            mb = 4 + k2
                            mm(banks[mb][:, :], blockones[:], sq_sb[k][:], True, True, [cB, sq_B[k]], [bankB[mb]], True)
                            act(lambda e, k=k, mb=mb: e.activation(out=r_sb[k][:], in_=banks[mb][:, :], func=AF.Ln, bias=cst_eps[:], scale=1.0),
                                reads=[bankB[mb], cB], writes=[r_B[k]])
                            act(lambda e, k=k: e.activation(out=r_sb[k][:], in_=r_sb[k][:], func=AF.Exp, scale=-0.5), reads=[r_B[k]], writes=[r_B[k]])
                            dve(lambda e, k=k, b=b, s=s, typ=typ: e.scalar_tensor_tensor(out=stg[s][:], in0=banks[b][:, :], scalar=qkn[:, typ:typ + 1], in1=r_sb[k][:],
                                                                                       op0=ALU.mult, op1=ALU.mult),
                                reads=[bankB[b], r_B[k], cB], writes=[stgB[s]])
                        elif typ == 2:
                            act(lambda e, b=b, s=s: e.activation(out=stg[s][:], in_=banks[b][:, :], func=AF.Copy, scale=0.125), reads=[bankB[b]], writes=[stgB[s]])
                        else:
                            dve(lambda e, b=b, s=s: e.tensor_copy(out=stg[s][:], in_=banks[b][:, :]), reads=[bankB[b]], writes=[stgB[s]])
                        P.dma("sp", XTg[g][c // 8].ap()[(c % 8) * 128:(c % 8 + 1) * 128, :], stg[s][:], stgD[s], reads=[stgB[s]], multi=[XTB[g][c // 8]])
                        if c % 8 == 7:
                            h2 = c // 8
                            o0 = (h2 * 4 + g) * 4096
                            (pending if h2 == 1 else pending_fox).append((XTB[g][h2], XTaB[h2], lambda e, g=g, h2=h2, o0=o0: e.collective_compute(
                                "AllGather", ALU.bypass, replica_groups=RG, ins=[XTg[g][h2].ap().opt()], outs=[XTa.ap()[o0:o0 + 4096, :].opt()])))

                def fl_part(g, xo, xb):
                    cols = slice(g * 512, (g + 1) * 512)
                    b = 6
                    for kc in range(8):
                        mm(banks[b][0:8, :], wfl_sb[:, kc * 8:(kc + 1) * 8], xnT[:, kc, xo:xo + 512], kc == 0, kc == 7, [wflB, xb], [bankB[b]], kc == 7)
                    act(lambda e, b=b: e.activation(out=fl_e[:], in_=banks[b][0:8, :], func=AF.Exp, bias=nfb[:], scale=-1.0), reads=[bankB[b], cB], writes=[flB])
                    act(lambda e: e.activation(out=fl_sp[:], in_=fl_e[:], func=AF.Ln, bias=1.0, scale=1.0), reads=[flB], writes=[flB])
                    P.dma("sp", XLv[:, cols], fl_sp[:], flD, reads=[flB], multi=[XLB])

                for i in range(4, 8):
                    norm_tile(i, tl8[i], 1, 0)
                for gi_, g in enumerate(gs):
                    qk_chunks(g, range(8, 16), gi_ * 512, xnTBs[gi_])
                    emit_v(g, gi_ * 512, xnTBs[gi_])
                flush_cc()
                for gi_, g in enumerate(gs):
                    qk_chunks(g, range(0, 8), gi_ * 512, xnTBs[gi_])
                    fl_part(g, gi_ * 512, xnTBs[gi_])
                flush_cc(fox=True)
                if last:
                    pending_fox.append((XLB, XLaB, lambda e: e.collective_compute(
                        "AllGather", ALU.bypass, replica_groups=RG, ins=[XL.ap().opt()], outs=[XLa.ap().opt()])))
                barrier(excl=CCX)
                put_ds(flD, *stgD, *wDs)

        with ExitStack() as st:
            ffn_phase(0, 0, st, g2s=(0,), excl=CCX)
        with ExitStack() as st:
            a2_phase(st, (0, 1), False)
        with ExitStack() as st:
            ffn_phase(0, 0, st, g2s=(1,), excl=CCX)
        dump("x1")
        with ExitStack() as st:
            a2_phase(st, (2, 3), True)
        if stop_after == "A2":
            finish()
            return nc

        def pid_r(e):
            if "r" not in _pid:
                _pid["r"] = e.partition_id() % 4
            return _pid["r"]

        ds_cc2 = get_ds()
        with ExitStack() as st:
            QT = [sb("QT%d" % i, [128, SEQ], BF16, st) for i in range(2)]
            KT = [sb("KT%d" % i, [128, SEQ], BF16, st) for i in range(2)]
            Vp = [sb("Vp%d" % i, [128, 65, 65], BF16, st) for i in range(2)]
            hbB = [Buf("headbuf%d" % i) for i in range(2)]
            hbD = [get_ds() for i in range(2)]
            PT = [sb("PT%d" % i, [128, 512], BF16, st) for i in range(3)]
            PTB = [Buf("PT%d" % i) for i in range(3)]
            et = [sb("et%d" % i, [128, 512], F32, st) for i in range(2)]
            etB = [Buf("et%d" % i) for i in range(2)]
            spt = [sb("spt%d" % i, [128, 512], BF16, st) for i in range(4)]
            sptB = [Buf("spt%d" % i) for i in range(4)]
            acc = [sb("acc%d" % i, [128, 512], BF16, st) for i in range(2)]
            accB = [Buf("acc%d" % i) for i in range(2)]
            rec = sb("rec", [128, 512], F32, st)
            recB = Buf("rec")
            bc_sb = rec
            bcB = Buf("bc_sb")
            ystg = [sb("ystg%d" % i, [64, 512], BF16, st) for i in range(2)]
            ystgB = [Buf("ystg%d" % i) for i in range(2)]
            ystgD = [get_ds() for i in range(2)]
            L_sb = sb("L_sb", [128, 2, 64], F32, st)
            Fs = sb("Fs", [128, 2, 64], F32, st)
            tot = sb("tot", [128, 2], F32, st)
            onesr = sb("onesr", [128, 64], F32, st)
            upper_f = sb("upper_f", [128, 128], F32, st)
            pcs = [sb("pcs%d" % i, [128, 2, 64], BF16, st) for i in range(3)]
            fB = Buf("fstuff")
            fD = get_ds()
            FPB = Buf("FP")
            YTB = [Buf("YT%d" % q) for q in range(4)]
            ZB = [0, 1, 2, 3]
            ZS = [0, 1, 2, 3, 6]
            YB = [4, 5]
            BCB = 6

            heads = [("sb", 0), ("sb", 1), ("fox", 0), ("fox", 1)]

            selB = [Buf("sel_fox"), Buf("sel_sb")]
            selD = get_ds()

            def do_select(h2):
                def selqk(e):
                    r = pid_r(e)
                    src = XTa.ap()[h2 * 16384:(h2 + 1) * 16384, :].rearrange("(m row) c -> row m c", row=512)[bass.ds(r * 128, 128), :, :]
                    return e.dma_start(out=MyQK[h2].ap(), in_=src)
                P.dma("sp", None, None, selD, reads=[XTaB[h2]], multi=[selB[h2]], fn=selqk)
                if h2 == 1:
                    def selv(e):
                        r = pid_r(e)
                        src = XVa.ap().rearrange("n (fs col) -> n fs col", fs=2)[:, :, bass.ds(r * 128, 128)]
                        return e.dma_start(out=MyV.ap(), in_=src)
                    P.dma("sp", None, None, selD, reads=[XVaB], multi=[selB[0], selB[1]], fn=selv)
                else:
                    def sell(e):
                        r = pid_r(e)
                        src = XLa.ap().rearrange("(rr h) t -> h rr t", rr=4)[bass.ds(r * 2, 2), :, :]
                        return e.dma_start(out=MyL.ap().rearrange("a (rr t) -> a rr t", rr=4), in_=src)
                    P.dma("sp", None, None, selD, reads=[XLaB], multi=[selB[0]], fn=sell)

            do_select(1)

            def load_head(n):
                kind, a = heads[n]
                hb = n % 2
                h2 = 0 if kind == "fox" else 1
                if kind == "fox":
                    pool(lambda e, hb=hb: e.memset(QT[hb][64:70, :], 1.0), writes=[hbB[hb]])
                    pool(lambda e, hb=hb: e.memset(KT[hb][64:70, :], -1.0), writes=[hbB[hb]])
                first = True
                for g in range(NG):
                    for kq, T in ((0, QT), (1, KT)):
                        src = MyQK[h2].ap()[a * 64:(a + 1) * 64, g * 8 + kq:g * 8 + kq + 7:2, :]
                        dst = T[hb][0:64, :].rearrange("p (rr g c) -> p g rr c", rr=4, g=4)[:, g]
                        if first:
                            P.dma("sp", dst, src, hbD[hb], reads=[selB[h2]], writes=[hbB[hb]])
                            first = False
                        else:
                            P.dma("sp", dst, src, hbD[hb], reads=[selB[h2]], multi=[hbB[hb]])
                    for rr in range(4):
                        n0 = (g * 4 + rr) * 512
                        blk = rr * 16 + g * 4
                        P.dma("sp", Vp[hb][:, blk:blk + 4, 0:64],
                              MyV.ap()[n0:n0 + 512, h2, a * 64:(a + 1) * 64].rearrange("(b p) d -> p b d", p=128),
                              hbD[hb], reads=[selB[h2]], multi=[hbB[hb]])
                if kind == "fox":
                    P.dma("sp", QT[hb][64:67, :], FP.ap()[a], hbD[hb], reads=[FPB], multi=[hbB[hb]])
                    P.dma("sp", KT[hb][67:70, :], FP.ap()[a], hbD[hb], reads=[FPB], multi=[hbB[hb]])

            for hb in range(2):
                pool(lambda e, hb=hb: e.memset(Vp[hb][:, 64, :], 0.0), writes=[hbB[hb]])
                pool(lambda e, hb=hb: e.memset(Vp[hb][:, :, 64:65], 1.0), writes=[hbB[hb]])
                pool(lambda e, hb=hb: e.memset(QT[hb][64:128, :], 0.0), writes=[hbB[hb]])
                pool(lambda e, hb=hb: e.memset(KT[hb][64:128, :], 0.0), writes=[hbB[hb]])
            pool(lambda e: e.memset(onesr[:], 1.0), writes=[fB])
            pool(lambda e: e.memset(upper_f[:], 1.0), writes=[fB])
            pool(lambda e: e.affine_select(out=upper_f[:], in_=upper_f[:], pattern=[[1, 128]], compare_op=ALU.is_gt, fill=0.0,
                                           base=0, channel_multiplier=-1), reads=[fB], writes=[fB])
            load_head(0)

            def tiles_of(kind):
                tl = []
                for g in range(16):
                    kbs = list(range(0, 4 * g + 4))
                    if kind == "sb":
                        kbs = kbs[::-1]
                    for idx, kb in enumerate(kbs):
                        j = kb - 4 * g
                        c0 = 128 * j if j >= 0 else 0
                        tl.append(dict(g=g, kb=kb, c0=c0, diag=(j >= 0), first=(idx == 0), last=(idx == len(kbs) - 1)))
                return tl

            ngroup = [0]
            nstg = [0]

            def evac(kind, n, a, t, yb):
                g = t["g"]
                s = nstg[0] % 2
                nstg[0] += 1
                Y = banks[yb]
                if kind == "fox":
                    dve(lambda e, Y=Y: e.reciprocal(out=rec[64:65, :], in_=Y[64:65, :]), reads=[bankB[yb]], writes=[recB])
                    mm(banks[BCB][0:64, :], ones_f[64:65, 0:64], rec[64:65, :], True, True, [cB, recB], [bankB[BCB]], True)
                    act(lambda e: e.copy(out=bc_sb[0:64, :], in_=banks[BCB][0:64, :]), reads=[bankB[BCB]], writes=[bcB])
                    dve(lambda e, Y=Y, s=s: e.tensor_tensor(out=ystg[s][:], in0=Y[0:64, :], in1=bc_sb[0:64, :], op=ALU.mult),
                        reads=[bankB[yb], bcB], writes=[ystgB[s]])
                else:
                    dve(lambda e, Y=Y, s=s: e.tensor_copy(out=ystg[s][:], in_=Y[0:64, :]), reads=[bankB[yb]], writes=[ystgB[s]])
                hi = (0 if kind == "fox" else 2) + a
                q4 = g // 4
                P.dma("sp", YTc[q4].ap()[hi * 64:(hi + 1) * 64, (g % 4) * 512:(g % 4 + 1) * 512], ystg[s][:], ystgD[s],
                      reads=[ystgB[s]], multi=[YTB[q4]])
                if n == 0 and g == 7:
                    for buf_, obuf_, fn_ in pending_fox:
                        P.dma("pool", None, None, ds_cc_fox, inc=1, reads=[buf_, ystgB[s]], multi=[obuf_], fn=fn_)
                    del pending_fox[:]
                if n == len(heads) - 1 and g % 4 == 3:
                    P.dma("pool", None, None, ds_cc2, inc=1, reads=[YTB[q4]],
                          fn=lambda e, q=q4: e.collective_compute("AllGather", ALU.bypass, replica_groups=RG,
                                                               ins=[YTc[q].ap().opt()], outs=[YTa.ap()[q * 1024:(q + 1) * 1024, :].opt()]))

            def run_fox(n, a):
                hb = n % 2
                tl = tiles_of("fox")
                nt = len(tl)
                LA = 2
                gy = {}
                for k in range(nt + LA):
                    if k < nt:
                        t = tl[k]
                        zb = ZB[k % 3]
                        c0 = t["c0"]
                        qc = t["g"] * 512
                        mm(banks[zb][:, c0:512], KT[hb][0:128, t["kb"] * 128:(t["kb"] + 1) * 128], QT[hb][0:128, qc + c0:qc + 512],
                           True, not t["diag"], [hbB[hb]], [bankB[zb]], not t["diag"])
                        if t["diag"]:
                            mm(banks[zb][:, c0:c0 + 128], ident[:], negmask_fox[:], False, True, [cB], [bankB[zb]], True)
                    j = k - LA
                    if j >= 0:
                        t = tl[j]
                        zb = ZB[j % 3]
                        pt = j % 3
                        c0 = t["c0"]
                        if t["first"]:
                            gy[t["g"]] = YB[ngroup[0] % 2]
                            ngroup[0] += 1
                        yb = gy[t["g"]]
                        act(lambda e, zb=zb, pt=pt, c0=c0: e.activation(out=PT[pt][:, c0:512], in_=banks[zb][:, c0:512], func=AF.Exp),
                            reads=[bankB[zb]], writes=[PTB[pt]])
                        mm(banks[yb][:, c0:512], Vp[hb][:].rearrange("p b d -> p (b d)")[:, t["kb"] * 65:t["kb"] * 65 + 128], PT[pt][:, c0:512], t["first"], t["last"],
                           [hbB[hb], PTB[pt]], [bankB[yb]], True)
                        if t["last"]:
                            evac("fox", n, a, t, yb)

            def run_sb(n, a):
                hb = n % 2
                tl = tiles_of("sb")
                nt = len(tl)
                gy = {}
                for i in range(nt + 4):
                    if i < nt:
                        t = tl[i]
                        zb = ZS[i % 5]
                        c0 = t["c0"]
                        qc = t["g"] * 512
                        mm(banks[zb][:, c0:512], KT[hb][0:128, t["kb"] * 128:(t["kb"] + 1) * 128], QT[hb][0:128, qc + c0:qc + 512],
                           True, False, [hbB[hb]], [bankB[zb]], not t["diag"])
                        if t["diag"]:
                            mm(banks[zb][:, c0:c0 + 128], ident[:], negmask_sb[:], False, False, [cB], [bankB[zb]], True)
                        ei = i % 2
                        act(lambda e, zb=zb, ei=ei, c0=c0: e.activation(out=et[ei][:, c0:512], in_=banks[zb][:, c0:512], func=AF.Exp),
                            reads=[bankB[zb]], writes=[etB[ei]])
                    j = i - 1
                    if 0 <= j < nt:
                        ei = j % 2
                        si = j % 4
                        c0 = tl[j]["c0"]
                        act(lambda e, ei=ei, si=si, c0=c0: e.activation(out=spt[si][:, c0:512], in_=et[ei][:, c0:512], func=AF.Ln, bias=1.0, scale=1.0),
                            reads=[etB[ei]], writes=[sptB[si]])
                    j = i - 3
                    if 0 <= j < nt:
                        t = tl[j]
                        zb = ZS[j % 5]
                        si = j % 4
                        c0 = t["c0"]
                        ai = j % 2
                        hasacc = not t["first"]
                        c1 = c0 + 128 if t["diag"] else 0
                        use_acc = hasacc and c1 < 512
                        mm(banks[zb][:, c0:512], negtri[:], spt[si][:, c0:512], False, not use_acc, [cB, sptB[si]], [bankB[zb]], not use_acc)
                        if use_acc:
                            mm(banks[zb][:, c1:512], negones[:], acc[ai][:, c1:512], False, True, [cB, accB[ai]], [bankB[zb]], True)
                        if not t["last"]:
                            if t["diag"]:
                                dve(lambda e, ai=ai, si=si, c0=c0: e.tensor_copy(out=acc[1 - ai][:, c0:c0 + 128], in_=spt[si][:, c0:c0 + 128]),
                                    reads=[sptB[si]], writes=[accB[1 - ai]])
                            if use_acc:
                                dve(lambda e, ai=ai, si=si, c1=c1: e.tensor_tensor(out=acc[1 - ai][:, c1:512], in0=acc[ai][:, c1:512], in1=spt[si][:, c1:512], op=ALU.add),
                                    reads=[accB[ai], sptB[si]], writes=[accB[1 - ai]])
                        pt = j % 3
                        act(lambda e, zb=zb, pt=pt, c0=c0: e.activation(out=PT[pt][:, c0:512], in_=banks[zb][:, c0:512], func=AF.Exp),
                            reads=[bankB[zb]], writes=[PTB[pt]])
                    j = i - 4
                    if 0 <= j < nt:
                        t = tl[j]
                        pt = j % 3
                        c0 = t["c0"]
                        if t["first"]:
                            gy[t["g"]] = YB[ngroup[0] % 2]
                            ngroup[0] += 1
                        yb = gy[t["g"]]
                        mm(banks[yb][:, c0:512], Vp[hb][:].rearrange("p b d -> p (b d)")[:, t["kb"] * 65:t["kb"] * 65 + 128], PT[pt][:, c0:512], t["first"], t["last"],
                           [hbB[hb], PTB[pt]], [bankB[yb]], True)
                        if t["last"]:
                            evac("sb", n, a, t, yb)

            def compute_F():
                P.dma("sp", L_sb[:], MyL.ap().rearrange("a (p j) -> p a j", p=128), fD, reads=[selB[0]], writes=[fB])
                for a_ in range(2):
                    dve(lambda e, a_=a_: e.tensor_tensor_scan(out=Fs[:, a_, :], data0=onesr[:, 0:64], data1=L_sb[:, a_, :], initial=0.0, op0=ALU.mult, op1=ALU.add),
                        reads=[fB], writes=[fB])
                dve(lambda e: e.tensor_copy(out=tot[:], in_=Fs[:, :, 63]), reads=[fB], writes=[fB])
                mm(banks[BCB][:, 0:2], upper_f[:], tot[:], True, True, [fB, cB], [bankB[BCB]], True)
                dve(lambda e: e.tensor_copy(out=tot[:], in_=banks[BCB][:, 0:2]), reads=[bankB[BCB]], writes=[fB])
                for a_ in range(2):
                    dve(lambda e, a_=a_: e.tensor_scalar(out=Fs[:, a_, :], in0=Fs[:, a_, :], scalar1=tot[:, a_:a_ + 1], scalar2=None, op0=ALU.add),
                        reads=[fB], writes=[fB])
                dve(lambda e: e.tensor_copy(out=pcs[0][:], in_=Fs[:]), reads=[fB], writes=[fB])
                dve(lambda e: e.tensor_tensor(out=L_sb[:], in0=Fs[:], in1=pcs[0][:], op=ALU.subtract), reads=[fB], writes=[fB])
                dve(lambda e: e.tensor_copy(out=pcs[1][:], in_=L_sb[:]), reads=[fB], writes=[fB])
                dve(lambda e: e.tensor_tensor(out=Fs[:], in0=L_sb[:], in1=pcs[1][:], op=ALU.subtract), reads=[fB], writes=[fB])
                dve(lambda e: e.tensor_copy(out=pcs[2][:], in_=Fs[:]), reads=[fB], writes=[fB])
                for k3 in range(3):
                    P.dma("sp", FP.ap()[:, k3, :].rearrange("a (p j) -> p a j", p=128), pcs[k3][:], fD, reads=[fB], multi=[FPB])


            nheads = int(os.environ.get("NHEADS", "4"))
            for n in range(nheads):
                kind, a = heads[n]
                if n + 1 < nheads:
                    load_head(n + 1)
                if kind == "fox":
                    run_fox(n, a)
                else:
                    run_sb(n, a)
                if n == 0:
                    do_select(0)
                    compute_F()
            barrier()
            put_ds(fD, selD, *hbD, *ystgD)

        if debug:
            ds = get_ds()
            for q in range(4):
                P.dma("sp", dbg["YT"][:, q * 2048:(q + 1) * 2048], YTc[q].ap(), ds)
            barrier()
        if stop_after == "B":
            finish()
            return nc

        ds_sel = get_ds()

        def sely(e):
            r = pid_r(e)
            return e.dma_start(out=MyY.ap(), in_=YTa.ap()[bass.ds(r * 1024, 1024), :])
        P.dma("sp", None, None, ds_sel, fn=sely)
        barrier()

        with ExitStack() as st:
            ytl = [sb("ytl%d" % i, [128, 8, 512], BF16, st) for i in range(2)]
            ytlB = [Buf("ytl%d" % i) for i in range(2)]
            ytlD = [get_ds() for i in range(2)]
            wbf_sb = sb("wbf_sb", [128, 4, 1024], BF16, st)
            wbs_sb = sb("wbs_sb", [128, 4, 1024], BF16, st)
            wout_sb = sb("wout_sb", [128, 8, 1024], BF16, st)
            wgt_sb = sb("wgt_sb", [128, 16, 1024], BF16, st)
            mT = sb("mT", [128, 8, 512], BF16, st)
            mTB = Buf("mT")
            sg = [sb("sg%d" % i, [128, 512], BF16, st) for i in range(2)]
            sgB = [Buf("sg%d" % i) for i in range(2)]
            tt = [sb("tt%d" % i, [128, 512], F32, st) for i in range(2)]
            ttB = [Buf("tt%d" % i) for i in range(2)]
            wgtB = [Buf("wgt%d" % i) for i in range(4)]
            wbfB, wbsB = Buf("wbf"), Buf("wbs")
            woutB = [Buf("wout%d" % i) for i in range(2)]
            wDs = [get_ds(True) for i in range(8)]
            yv = MyY.ap().rearrange("(ch p) t -> p ch t", p=128)
            norm_T(0, 1)

            def ld_wgt(q):
                P.dma("pool", wgt_sb[:, 4 * q:4 * q + 4, :], wgt_d[4 * q:4 * q + 4].rearrange("c p n -> p c n"), wDs[q], writes=[wgtB[q]])
            ld_wgt(0)
            ld_wgt(2)
            P.dma("pool", wbf_sb[:].rearrange("p a n -> p (a n)"), wbf_d, wDs[4], writes=[wbfB])
            P.dma("pool", wbs_sb[:].rearrange("p a n -> p (a n)"), wbs_d, wDs[5], writes=[wbsB])
            ld_wgt(1)
            ld_wgt(3)
            for h in range(2):
                P.dma("pool", wout_sb[:, 4 * h:4 * h + 4, :].rearrange("p a n -> p (a n)"), wout_d[:, h * 4096:(h + 1) * 4096], wDs[6 + h], writes=[woutB[h]])
            for g in range(NG):
                yb_ = g % 2
                P.dma("sp", ytl[yb_][:], yv[:, :, g * 512:(g + 1) * 512], ytlD[yb_], writes=[ytlB[yb_]])
                xo = (g % 2) * 512
                xb = xnTBs[g % 2]
                nxt = [4 * (g + 1) + i for i in range(4)] if g + 1 < NG else []
                if nxt:
                    norm_stats(nxt)
                for c in range(8):
                    gf, gs = (0, 1) if c % 2 == 0 else (2, 3)
                    zf, zs = 4, 5
                    for kc in range(8):
                        mm(banks[gf][:, :], wgt_sb[:, c, kc * 128:(kc + 1) * 128], xnT[:, kc, xo:xo + 512], kc == 0, kc == 7, [wgtB[c // 4], xb], [bankB[gf]], kc == 7)
                    for kc in range(8):
                        mm(banks[gs][:, :], wgt_sb[:, 8 + c, kc * 128:(kc + 1) * 128], xnT[:, kc, xo:xo + 512], kc == 0, kc == 7, [wgtB[2 + c // 4], xb], [bankB[gs]], kc == 7)
                    for rr in range(4):
                        mm(banks[zf][:, :], wbf_sb[:, rr, c * 128:(c + 1) * 128], ytl[yb_][:, rr * 2, :], rr == 0, rr == 3, [wbfB, ytlB[yb_]], [bankB[zf]], rr == 3)
                    for rr in range(4):
                        mm(banks[zs][:, :], wbs_sb[:, rr, c * 128:(c + 1) * 128], ytl[yb_][:, rr * 2 + 1, :], rr == 0, rr == 3, [wbsB, ytlB[yb_]], [bankB[zs]], rr == 3)
                    act(lambda e, gf=gf: e.activation(out=sg[0][:], in_=banks[gf][:, :], func=AF.Sigmoid), reads=[bankB[gf]], writes=[sgB[0]])
                    act(lambda e, gs=gs: e.activation(out=sg[1][:], in_=banks[gs][:, :], func=AF.Sigmoid), reads=[bankB[gs]], writes=[sgB[1]])
                    dve(lambda e, zf=zf: e.tensor_tensor(out=tt[0][:], in0=banks[zf][:, :], in1=sg[0][:], op=ALU.mult), reads=[bankB[zf], sgB[0]], writes=[ttB[0]])
                    dve(lambda e, zs=zs: e.tensor_tensor(out=tt[1][:], in0=banks[zs][:, :], in1=sg[1][:], op=ALU.mult), reads=[bankB[zs], sgB[1]], writes=[ttB[1]])
                    pool(lambda e, c=c: e.tensor_tensor(out=mT[:, c, :], in0=tt[0][:], in1=tt[1][:], op=ALU.add), reads=[ttB[0], ttB[1]], writes=[mTB])
                    if nxt and c % 2 == 1:
                        norm_tile(c // 2, nxt[c // 2], 1, off=((g + 1) % 2) * 512)
                for i in range(4):
                    t = 4 * g + i
                    for j in range(2):
                        b = 6
                        for c in range(8):
                            mm(banks[b][:, :], mT[:, c, i * 128:(i + 1) * 128], wout_sb[:, c, j * 512:(j + 1) * 512], c == 0, c == 7, [mTB, woutB[c // 4]], [bankB[b]], c == 7)
                        dve(lambda e, b=b, t=t, j=j: e.tensor_tensor(out=xres[:, t, j * 512:(j + 1) * 512], in0=banks[b][:, :], in1=xres[:, t, j * 512:(j + 1) * 512], op=ALU.add),
                            reads=[bankB[b], xB[t]], writes=[xB[t]])
            barrier()
            put_ds(*wDs, *ytlD)
        dump("x2")
        if stop_after == "C1":
            finish()
            return nc

        with ExitStack() as st:
            ffn_phase(1, 2, st)
        dump("x3")

        with ExitStack() as st:
            wpg_sb = sb("wpg_sb", [128, 8, 1024], BF16, st)
            wpp_sb = sb("wpp_sb", [128, 2, 1024], BF16, st)
            wB = Buf("wC3")
            ds_w3 = [get_ds(True) for i in range(3)]
            norm_T(0, 3)
            for h in range(2):
                P.dma("pool", wpg_sb[:, 4 * h:4 * h + 4, :].rearrange("p a n -> p (a n)"), wpg_d[:, h * 4096:(h + 1) * 4096], ds_w3[h], multi=[wB])
            P.dma("pool", wpp_sb[:].rearrange("p a n -> p (a n)"), wpp_d, ds_w3[2], multi=[wB])
            p_b = [sb("p_b%d" % i, [128, 256], BF16, st) for i in range(2)]
            p_B = [Buf("p_b%d" % i) for i in range(2)]
            p_D = [get_ds(True) for i in range(2)]
            pT = sb("pT", [128, 2, 512], BF16, st)
            pTB = Buf("pT")
            sg = [sb("sgp%d" % i, [128, 512], F32, st) for i in range(2)]
            sgB = [Buf("sgp%d" % i) for i in range(2)]
            tt = [sb("ttp%d" % i, [128, 512], F32, st) for i in range(2)]
            ttB = [Buf("ttp%d" % i) for i in range(2)]
            ostg = [sb("ostg%d" % i, [128, 512], F32, st) for i in range(3)]
            ostgB = [Buf("ostg%d" % i) for i in range(3)]
            ostgD = [get_ds() for i in range(3)]
            no = 0
            for g in range(NG):
                xo = (g % 2) * 512
                xb = xnTBs[g % 2]
                nxt = [4 * (g + 1) + i for i in range(4)] if g + 1 < NG else []
                if nxt:
                    norm_stats(nxt)
                for i in range(4):
                    t = 4 * g + i
                    k = i % 2
                    P.dma("pool", p_b[k][:], p_d[t * 128:(t + 1) * 128, :], p_D[k], writes=[p_B[k]])
                    for k2 in range(2):
                        P.pe_open = 0 if k2 == 1 else 1
                        P.op("pe", lambda e, k=k, k2=k2: e.transpose(out=trps[:, k2 * 128:(k2 + 1) * 128], in_=p_b[k][:, k2 * 128:(k2 + 1) * 128], identity=ident[:]),
                             reads=[p_B[k]], writes=[trB], sig=(k2 == 1))
                    act(lambda e, i=i: e.copy(out=pT[:, :, i * 128:(i + 1) * 128], in_=trps[:, 0:256].rearrange("p (k c) -> p k c", c=128)),
                        reads=[trB], writes=[pTB])
                for i in range(4):
                    t = 4 * g + i
                    for j in range(2):
                        u = (2 * i + j) % 2
                        gb, pb = (0, 2) if u == 0 else (1, 3)
                        for kc in range(8):
                            mm(banks[gb][:, :], xnT[:, kc, xo + i * 128:xo + (i + 1) * 128], wpg_sb[:, kc, j * 512:(j + 1) * 512], kc == 0, kc == 7, [xb, wB], [bankB[gb]], kc == 7)
                        for k2 in range(2):
                            mm(banks[pb][:, :], pT[:, k2, i * 128:(i + 1) * 128], wpp_sb[:, k2, j * 512:(j + 1) * 512], k2 == 0, k2 == 1, [pTB, wB], [bankB[pb]], k2 == 1)
                        act(lambda e, gb=gb, u=u: e.activation(out=sg[u][:], in_=banks[gb][:, :], func=AF.Sigmoid), reads=[bankB[gb]], writes=[sgB[u]])
                        dve(lambda e, pb=pb, u=u: e.tensor_tensor(out=tt[u][:], in0=banks[pb][:, :], in1=sg[u][:], op=ALU.mult), reads=[bankB[pb], sgB[u]], writes=[ttB[u]])
                        o = no % 3
                        no += 1
                        pool(lambda e, o=o, u=u, t=t, j=j: e.tensor_tensor(out=ostg[o][:], in0=tt[u][:], in1=xres[:, t, j * 512:(j + 1) * 512], op=ALU.add),
                             reads=[ttB[u], xB[t]], writes=[ostgB[o]])
                        P.dma("sp", out_d[t * 128:(t + 1) * 128, j * 512:(j + 1) * 512], ostg[o][:], ostgD[o], reads=[ostgB[o]])
                        if nxt and j == 1:
                            norm_tile(i, nxt[i], 3, off=((g + 1) % 2) * 512)
            barrier()
        finish()
    return nc


def _tile_cols(w, c0, ncols, cw):
    sub = w[:, c0:c0 + ncols]
    k = sub.shape[0] // 128
    t = sub.reshape(k, 128, ncols // cw, cw)
    return np.ascontiguousarray(t.transpose(2, 1, 0, 3).reshape(ncols // cw, 128, k * cw))


def _tile_rows(w):
    r = w.shape[0] // 128
    return np.ascontiguousarray(w.reshape(r, 128, w.shape[1]).transpose(1, 0, 2).reshape(128, r * w.shape[1]))


def prep_inputs(inp):
    f = lambda a: np.asarray(a, dtype=np.float32)
    x = f(inp["x"]).reshape(8, TOK, D)
    p = f(inp["p"])[0].reshape(8, TOK, 256)
    w_in = f(inp["w_in"])[0]
    shared = {
        "wg1": _tile_cols(f(inp["ffn1_w_gate"])[0], 0, DFF, 128),
        "wu1": _tile_cols(f(inp["ffn1_w_up"])[0], 0, DFF, 128),
        "wd1": _tile_rows(f(inp["ffn1_w_down"])[0]),
        "wg2": _tile_cols(f(inp["ffn2_w_gate"])[0], 0, DFF, 128),
        "wu2": _tile_cols(f(inp["ffn2_w_up"])[0], 0, DFF, 128),
        "wd2": _tile_rows(f(inp["ffn2_w_down"])[0]),
        "wqk": np.concatenate([_tile_cols(w_in, C_FQ, 512, 128), _tile_cols(w_in, C_FK, 512, 128),
                               _tile_cols(w_in, C_SQ, 512, 128), _tile_cols(w_in, C_SK, 512, 128)], 0),
        "wv": np.concatenate([_tile_cols(w_in, C_FV, 512, 512), _tile_cols(w_in, C_SV, 512, 512)], 0),
        "wfl": _tile_cols(w_in, C_FL, 8, 8)[0],
        "wgt": np.concatenate([_tile_cols(w_in, C_GF, 1024, 128), _tile_cols(w_in, C_GS, 1024, 128)], 0),
        "wbf": _tile_rows(f(inp["w_branch_fox"])[0]),
        "wbs": _tile_rows(f(inp["w_branch_sb"])[0]),
        "wout": _tile_rows(f(inp["w_out"])[0]),
        "wpg": _tile_rows(f(inp["w_ple_gate"])[0]),
        "wpp": _tile_rows(f(inp["w_ple_proj"])[0]),
        "gains": np.ascontiguousarray(np.stack([f(inp["ffn1_norm"])[0], f(inp["mix_norm"])[0], f(inp["ffn2_norm"])[0], f(inp["ple_norm"])[0]], 0)),
        "fb": np.ascontiguousarray(f(inp["forget_bias"])[0].reshape(8, 1)),
        "qkn": np.ascontiguousarray(np.stack([np.tile(f(inp["q_norm"])[0], 2), np.tile(f(inp["k_norm"])[0], 2)], 1)),
    }
    maps = []
    for c in range(8):
        m = dict(shared)
        m["x"] = np.ascontiguousarray(x[c])
        m["p"] = np.ascontiguousarray(p[c])
        maps.append(m)
    return maps


_NC_CACHE = {}


def kernel(**inputs):
    if "nc" not in _NC_CACHE:
        _NC_CACHE["nc"] = build_program()
    nc = _NC_CACHE["nc"]
    maps = prep_inputs(inputs)
    res = run_bass_kernel_spmd(nc, maps, core_ids=list(range(8)))
    out = np.stack([np.asarray(r["out"], dtype=np.float32) for r in res.results], 0)
    return out.reshape(2, SEQ, D)
```
